# Optimizing a Trainium2 kernel written in Bass

```python
import math, functools
import jax, jax.numpy as jnp
from jax import lax
import numpy as np

D_MODEL = 1024
BATCH = 4
SEQ = 4096
DEPTH = 4
DEC_BATCH = 8
DEC_SEQ = 8192
PAST_LEN = 128

FNET_WIDTH = D_MODEL
FNET_GROUPS = 4
FNET_GROUP_DIM = FNET_WIDTH // FNET_GROUPS
RET_HEAD_QK = 256
RET_HEADS = D_MODEL // RET_HEAD_QK
RET_HEAD_V = 2 * RET_HEAD_QK
RET_QK = RET_HEADS * RET_HEAD_QK
RET_V = RET_HEADS * RET_HEAD_V
RET_CHUNK = 128
ROPE_BASE = 10000.0
CONV_WIDTH = D_MODEL
CONV_K = 3
FFN_HIDDEN = -(-8 * D_MODEL // (3 * 256)) * 256
DEEPNORM_ALPHA = (2.0 * DEPTH) ** 0.25
DEEPNORM_BETA = (8.0 * DEPTH) ** -0.25
LN_EPS = 1e-5
HEAD_NORM_EPS = 1e-6
IN_SPLITS = (FNET_WIDTH, RET_QK, RET_QK, RET_V, RET_V,
             CONV_WIDTH, CONV_WIDTH, CONV_WIDTH, D_MODEL, D_MODEL, D_MODEL)
IN_COLS = sum(IN_SPLITS)

kernel_name = "hybrid_fnet_retention_shortconv_encoder"


def _layer_norm(x, gain, bias):
    xf = x.astype(jnp.float32)
    mu = jnp.mean(xf, axis=-1, keepdims=True)
    xc = xf - mu
    var = jnp.mean(xc * xc, axis=-1, keepdims=True)
    y = xc * lax.rsqrt(var + LN_EPS) * gain.astype(jnp.float32) + bias.astype(jnp.float32)
    return y.astype(x.dtype)


def _rope_tables(seq_len):
    inv_freq = 1.0 / (ROPE_BASE ** jnp.linspace(0.0, 1.0, RET_HEAD_QK // 2, dtype=jnp.float32))
    ang = jnp.arange(seq_len, dtype=jnp.float32)[:, None] * inv_freq[None, :]
    return jnp.cos(ang)[:, None, :], jnp.sin(ang)[:, None, :]


def _rope(t, cos, sin):
    t1, t2 = jnp.split(t, 2, axis=-1)
    return jnp.concatenate([t1 * cos - t2 * sin, t1 * sin + t2 * cos], axis=-1)


def _fourier_mix(u):
    B, S, _ = u.shape
    ug = u.astype(jnp.float32).reshape(B, S, FNET_GROUPS, FNET_GROUP_DIM)
    f = jnp.fft.fft2(ug, axes=(1, 3), norm="ortho").real
    return f.reshape(B, S, FNET_WIDTH).astype(u.dtype)


def _retention_causal(qc, kc, vc, log_gamma, strict):
    _, B, H, C, dk = qc.shape
    dv = vc.shape[-1]
    pos = jnp.arange(C, dtype=jnp.float32)
    diff = pos[:, None] - pos[None, :]
    mask = (diff > 0) if strict else (diff >= 0)
    lg = log_gamma[:, None, None]
    d_intra = jnp.where(mask, jnp.exp(jnp.where(mask, diff, 0.0) * lg), 0.0)
    d_q = jnp.exp((pos + 1.0) * lg[:, :, 0])[:, :, None]
    d_k = jnp.exp((C - 1.0 - pos) * lg[:, :, 0])[:, :, None]
    d_chunk = jnp.exp(C * log_gamma)[:, None, None]

    def step(state, blk):
        q, k, v = blk
        scores = jnp.einsum("bhid,bhjd->bhij", q, k) * d_intra
        o = (jnp.einsum("bhij,bhjv->bhiv", scores, v)
             + jnp.einsum("bhid,bhdv->bhiv", q * d_q, state))
        state = d_chunk * state + jnp.einsum("bhjd,bhjv->bhdv", k * d_k, v)
        return state, o

    state0 = jnp.zeros((B, H, dk, dv), jnp.float32)
    _, out = lax.scan(step, state0, (qc, kc, vc))
    return out


def _retention(q, k, v, g, decay_logit, cos, sin):
    B, S, _ = q.shape
    dt = q.dtype
    nc = S // RET_CHUNK
    qh = _rope(q.astype(jnp.float32).reshape(B, S, RET_HEADS, RET_HEAD_QK), cos, sin)
    kh = _rope(k.astype(jnp.float32).reshape(B, S, RET_HEADS, RET_HEAD_QK), cos, sin) * (RET_HEAD_QK ** -0.5)
    vh = v.astype(jnp.float32).reshape(B, S, RET_HEADS, RET_HEAD_V)

    def chunked(t):
        return t.reshape(B, nc, RET_CHUNK, RET_HEADS, t.shape[-1]).transpose(1, 0, 3, 2, 4)

    qc, kc, vc = chunked(qh), chunked(kh), chunked(vh)
    log_gamma = jax.nn.log_sigmoid(decay_logit.astype(jnp.float32))

    def flip(t):
        return jnp.flip(t, axis=(0, 3))

    fwd = _retention_causal(qc, kc, vc, log_gamma[0], strict=False)
    bwd = flip(_retention_causal(flip(qc), flip(kc), flip(vc), log_gamma[1], strict=True))
    o = (fwd + bwd).transpose(1, 0, 3, 2, 4).reshape(B, S, RET_HEADS, RET_HEAD_V)
    mu = jnp.mean(o, axis=-1, keepdims=True)
    oc = o - mu
    o = oc * lax.rsqrt(jnp.mean(oc * oc, axis=-1, keepdims=True) + HEAD_NORM_EPS)
    o = o.reshape(B, S, RET_V)
    return (jax.nn.silu(g.astype(jnp.float32)) * o).astype(dt)


def _short_conv(b, c, xv, conv_w):
    u = c * xv
    up = jnp.pad(u, ((0, 0), (1, 1), (0, 0)))
    y = conv_w[0] * up[:, :-2] + conv_w[1] * up[:, 1:-1] + conv_w[2] * up[:, 2:]
    return b * y


def _mixer(x, w_in_l, decay_logit_l, conv_w_l, w_f_out, w_r_out, w_c_out, w_o_l, cos, sin):
    (wf, wq, wk, wv, wg, wcb, wcc, wcx, wgf, wgr, wgc) = jnp.split(
        w_in_l, np.cumsum(IN_SPLITS)[:-1], axis=1)

    def proj(w):
        return jnp.einsum("bsd,de->bse", x, w)

    f = jnp.einsum("bse,ed->bsd", _fourier_mix(proj(wf)), w_f_out)
    r = jnp.einsum("bse,ed->bsd",
                   _retention(proj(wq), proj(wk), proj(wv), proj(wg), decay_logit_l, cos, sin), w_r_out)
    c = jnp.einsum("bse,ed->bsd", _short_conv(proj(wcb), proj(wcc), proj(wcx), conv_w_l), w_c_out)
    merged = (jax.nn.sigmoid(proj(wgf)) * f
              + jax.nn.sigmoid(proj(wgr)) * r
              + jax.nn.sigmoid(proj(wgc)) * c)
    return jnp.einsum("bsd,de->bse", merged, w_o_l)


def _ffn(x, w_ffn_in_l, w_ffn_out_l):
    gate, up = jnp.split(jnp.einsum("bsd,df->bsf", x, w_ffn_in_l), 2, axis=-1)
    return jnp.einsum("bsf,fd->bsd", jax.nn.silu(gate) * up, w_ffn_out_l)


def _trunk(x, w_in, ret_decay_logit, conv_w, w_fourier_out, w_ret_out, w_conv_out, w_o,
           ln_gain, ln_bias, w_ffn_in, w_ffn_out):
    cos, sin = _rope_tables(x.shape[1])
    for l in range(DEPTH):
        mix = _mixer(x, w_in[l], ret_decay_logit[l], conv_w[l], w_fourier_out[l], w_ret_out[l],
                     w_conv_out[l], w_o[l], cos, sin)
        x = _layer_norm(DEEPNORM_ALPHA * x + mix, ln_gain[l, 0], ln_bias[l, 0])
        x = _layer_norm(DEEPNORM_ALPHA * x + _ffn(x, w_ffn_in[l], w_ffn_out[l]), ln_gain[l, 1], ln_bias[l, 1])
    return x


def setup_inputs(seed: int = 0) -> dict:
    key = jax.random.key(seed)
    ks = jax.random.split(key, 14)
    f32 = jnp.float32
    gamma = 1.0 - 2.0 ** (-5.0 - jnp.arange(RET_HEADS, dtype=f32))
    base_logit = jnp.log(gamma) - jnp.log1p(-gamma)
    return {
        "x_prompt": jax.random.normal(ks[0], (BATCH, SEQ, D_MODEL), f32),
        "x_sample": jax.random.normal(ks[1], (DEC_BATCH, DEC_SEQ, D_MODEL), f32),
        "w_in": jax.random.normal(ks[2], (DEPTH, D_MODEL, IN_COLS), f32) * D_MODEL ** -0.5,
        "ret_decay_logit": base_logit[None, None, :] + 0.1 * jax.random.normal(ks[3], (DEPTH, 2, RET_HEADS), f32),
        "conv_w": jax.random.normal(ks[4], (DEPTH, CONV_K, CONV_WIDTH), f32) * CONV_K ** -0.5,
        "w_fourier_out": jax.random.normal(ks[5], (DEPTH, FNET_WIDTH, D_MODEL), f32) * FNET_WIDTH ** -0.5,
        "w_ret_out": jax.random.normal(ks[6], (DEPTH, RET_V, D_MODEL), f32) * RET_V ** -0.5,
        "w_conv_out": jax.random.normal(ks[7], (DEPTH, CONV_WIDTH, D_MODEL), f32) * CONV_WIDTH ** -0.5,
        "w_o": jax.random.normal(ks[8], (DEPTH, D_MODEL, D_MODEL), f32) * (D_MODEL ** -0.5 * DEEPNORM_BETA),
        "ln_gain": 1.0 + 0.02 * jax.random.normal(ks[9], (DEPTH, 2, D_MODEL), f32),
        "ln_bias": 0.02 * jax.random.normal(ks[10], (DEPTH, 2, D_MODEL), f32),
        "w_ffn_in": jax.random.normal(ks[11], (DEPTH, D_MODEL, 2 * FFN_HIDDEN), f32) * D_MODEL ** -0.5,
        "w_ffn_out": jax.random.normal(ks[12], (DEPTH, FFN_HIDDEN, D_MODEL), f32) * (FFN_HIDDEN ** -0.5 * DEEPNORM_BETA),
    }


def reference(x_prompt, x_sample, w_in, ret_decay_logit, conv_w, w_fourier_out, w_ret_out,
              w_conv_out, w_o, ln_gain, ln_bias, w_ffn_in, w_ffn_out):
    y_prompt = _trunk(x_prompt, w_in, ret_decay_logit, conv_w, w_fourier_out, w_ret_out, w_conv_out,
                      w_o, ln_gain, ln_bias, w_ffn_in, w_ffn_out)
    y_sample = _trunk(x_sample, w_in, ret_decay_logit, conv_w, w_fourier_out, w_ret_out, w_conv_out,
                      w_o, ln_gain, ln_bias, w_ffn_in, w_ffn_out)
    return (y_prompt, y_sample)
```

```python
import contextlib
import math
import numpy as np
import ml_dtypes
import concourse.bass as bass
import concourse.mybir as mybir
from concourse.bass_utils import run_bass_kernel_spmd

F32 = mybir.dt.float32
BF16 = mybir.dt.bfloat16
AF = mybir.ActivationFunctionType
ALU = mybir.AluOpType

D = 1024
NH = 4
DK = 256
DV = 512
RV = NH * DV
FF = 2816
INC = 13312
C_F, C_Q, C_K, C_V, C_G = 0, 1024, 2048, 3072, 5120
C_CB, C_CC, C_CX, C_GF, C_GR, C_GC = 7168, 8192, 9216, 10240, 11264, 12288
DEPTH_REF = 4
ALPHA = (2.0 * DEPTH_REF) ** 0.25
LN_EPS = 1e-5
HN_EPS = 1e-6
TT = 512


class Res:
    __slots__ = ("w", "r", "base", "name")

    def __init__(self, name=""):
        self.w = {}
        self.r = {}
        self.base = {}
        self.name = name


class Eng:
    def __init__(self, name, h, sem):
        self.name = name
        self.h = h
        self.sem = sem
        self.cnt = 0
        self.seen = {}


class Prog:
    NDMA = 20

    def __init__(self, nc, stack):
        self.nc = nc
        self.sems = []

        def mk(nm):
            s = stack.enter_context(nc.semaphore(nm))
            self.sems.append(s)
            return len(self.sems) - 1

        self.E = {}
        for nm, h in (("pe", nc.tensor), ("act", nc.scalar), ("dve", nc.vector),
                      ("pool", nc.gpsimd), ("sp", nc.sync)):
            self.E[nm] = Eng(nm, h, mk("s_" + nm))
        self.dq = {}
        for q, e in (("sp", "sp"), ("bg", "pool")):
            self.dq[q] = dict(sems=[mk(f"d_{q}{i}") for i in range(self.NDMA)], n=0, eng=e)
        self.ninst = 0
        self.npe = 0
        self.plog = []

    def _need(self, reads, writes, pwrites):
        need = {}
        for res in reads:
            for s, v in res.w.items():
                if need.get(s, 0) < v:
                    need[s] = v
        for res in writes:
            for d in (res.w, res.r):
                for s, v in d.items():
                    if need.get(s, 0) < v:
                        need[s] = v
        for res in pwrites:
            for d in (res.r, res.base):
                for s, v in d.items():
                    if need.get(s, 0) < v:
                        need[s] = v
        return need

    def _waits(self, eng, need):
        seen = eng.seen
        for s, v in need.items():
            if seen.get(s, 0) < v:
                eng.h.wait_ge(self.sems[s], v)
                seen[s] = v
                self.ninst += 1

    def _mark(self, tk, reads, writes, pwrites):
        s, v = tk
        for res in reads:
            res.r[s] = v
        for res in writes:
            base = dict(res.w)
            for s2, v2 in res.r.items():
                if base.get(s2, 0) < v2:
                    base[s2] = v2
            res.base = base
            res.w = {s: v}
            res.r = {}
        for res in pwrites:
            res.w[s] = v

    def op(self, en, fn, reads=(), writes=(), pwrites=()):
        eng = self.E[en]
        self._waits(eng, self._need(reads, writes, pwrites))
        ins = fn(eng.h)
        eng.cnt += 1
        ins.then_inc(self.sems[eng.sem], 1)
        self.ninst += 1
        self._mark((eng.sem, eng.cnt), reads, writes, pwrites)

    def mm(self, fns, reads=(), writes=(), pwrites=()):
        eng = self.E["pe"]
        self._waits(eng, self._need(reads, writes, pwrites))
        ins = None
        for fn in fns:
            ins = fn(eng.h)
            self.ninst += 1
            self.npe += 1
        eng.cnt += 1
        ins.then_inc(self.sems[eng.sem], 1)
        self._mark((eng.sem, eng.cnt), reads, writes, pwrites)

    def dma(self, q, out, in_, reads=(), writes=(), pwrites=(), **kw):
        pool = self.dq[q]
        eng = self.E[pool["eng"]]
        i = pool["n"]
        pool["n"] += 1
        s = pool["sems"][i % self.NDMA]
        v = 16 * (i // self.NDMA + 1)
        need = self._need(reads, writes, pwrites)
        if v > 16 and need.get(s, 0) < v - 16:
            need[s] = v - 16
        self._waits(eng, need)
        eng.h.dma_start(out=out, in_=in_, **kw).then_inc(self.sems[s], 16)
        self.ninst += 1
        self._mark((s, v), reads, writes, pwrites)

    def barrier(self, include_bg=False, label=None):
        if label:
            self.plog.append((label, self.npe))
        latest = {}
        for e in self.E.values():
            if e.cnt:
                latest[e.sem] = e.cnt
        for q, pool in self.dq.items():
            if q == "bg" and not include_bg:
                continue
            n = pool["n"]
            for k, s in enumerate(pool["sems"]):
                c = (n - k + self.NDMA - 1) // self.NDMA if n > k else 0
                if c:
                    latest[s] = 16 * c
        for e in self.E.values():
            self._waits(e, latest)


class T:
    uid = [0]

    def __init__(self, nc, st, name, shape, dt, nres=1):
        T.uid[0] += 1
        name = f"sb{T.uid[0]}_{name}"
        self.t = st.enter_context(nc.sbuf_tensor(name, list(shape), dt))
        self.R = Res(name)
        self.Rs = [Res(f"{name}{i}") for i in range(nres)] if nres > 1 else [self.R]


def _bf(a):
    return np.ascontiguousarray(a.astype(np.float32)).astype(ml_dtypes.bfloat16)


def make_consts(seq_lens, smax):
    c = {}
    c["ident"] = _bf(np.eye(128))
    j = np.arange(128)[:, None]
    i = np.arange(128)[None, :]
    c["mask"] = np.stack([(i >= j), (j > i)]).astype(np.float32)
    p = np.arange(128, dtype=np.float64)
    c["pcols"] = np.stack([p + 1, -(p + 1), 128 - p, p - 128], axis=1).astype(np.float32)
    a = 2 * np.pi * np.outer(np.arange(256), np.arange(256)) / 256.0
    c["chan"] = _bf(np.concatenate([np.cos(a), -np.sin(a)], axis=1) / 16.0)
    for S in sorted(set(seq_lens)):
        n2c = S // 128
        n1 = np.arange(128)[:, None]
        k1 = np.arange(128)[None, :]
        mats = np.zeros((n2c, 128, 3, 128), np.float64)
        for n2 in range(n2c):
            ang = 2 * np.pi * (n1 * k1 / 128.0 + n2 * k1 / float(S))
            mats[n2, :, 0] = np.cos(ang)
            mats[n2, :, 1] = np.sin(ang)
            mats[n2, :, 2] = -np.sin(ang)
        c[f"t1_{S}"] = _bf(mats / math.sqrt(128.0))
        a2 = 2 * np.pi * np.outer(np.arange(n2c), np.arange(n2c)) / float(n2c)
        c[f"t3_{S}"] = _bf(np.concatenate([np.cos(a2), np.sin(a2)], axis=0) / math.sqrt(n2c))
    inv = 1.0 / (10000.0 ** np.linspace(0.0, 1.0, 128, dtype=np.float32))
    ang = inv[:, None].astype(np.float32) * np.arange(smax, dtype=np.float32)[None, :]
    c["cos"] = np.cos(ang).astype(np.float32)
    c["sin"] = np.sin(ang).astype(np.float32)
    return c


def build(seqs, L, debug=False):
    nc = bass.Bass("TRN2", target_bir_lowering=False)
    smax = max(S for _, S in seqs)

    def din(name, shape, dt=F32):
        return nc.dram_tensor(name, list(shape), dt, kind="ExternalInput").ap()

    def dscr(name, shape, dt):
        return nc.dram_tensor(name, list(shape), dt,
                              kind="ExternalOutput" if debug else "Internal").ap()

    xin = {nm: din(f"x_{nm}", [S, D]) for nm, S in seqs}
    yout = {nm: nc.dram_tensor(f"y_{nm}", [S, D], F32, kind="ExternalOutput").ap() for nm, S in seqs}
    wsrc = {
        "in": din("w_in", [L, D, INC]), "fo": din("w_fourier_out", [L, D, D]),
        "ro": din("w_ret_out", [L, RV, D]), "co": din("w_conv_out", [L, D, D]),
        "o": din("w_o", [L, D, D]), "fi": din("w_ffn_in", [L, D, 2 * FF]),
        "fout": din("w_ffn_out", [L, FF, D]),
    }
    decay_in = din("ret_decay_logit", [L, 8])
    convw_in = din("conv_w", [L, 3, D])
    lng_in = din("ln_gain", [L, 2, D])
    lnb_in = din("ln_bias", [L, 2, D])
    c_ident = din("ident", [128, 128], BF16)
    c_mask = din("mask", [2, 128, 128])
    c_pcols = din("pcols", [128, 4])
    c_chan = din("chan", [256, 512], BF16)
    c_cos = din("cos", [128, smax])
    c_sin = din("sin", [128, smax])
    c_t1, c_t3 = {}, {}
    for S in sorted(set(S for _, S in seqs)):
        c_t1[S] = din(f"t1_{S}", [S // 128, 128, 3, 128], BF16)
        c_t3[S] = din(f"t3_{S}", [2 * (S // 128), S // 128], BF16)

    wb = {k: dscr("wb_" + k, v.shape, BF16) for k, v in wsrc.items()}
    wbR = {k: [Res(f"wb_{k}{l}") for l in range(L)] for k in wsrc}
    scr = {}
    for nm, S in seqs:
        n2c = S // 128
        scr[nm] = dict(
            xT=[dscr(f"xT{i}_{nm}", [D, S], BF16) for i in range(2)],
            xres=[dscr(f"xres{i}_{nm}", [S, D], F32) for i in range(2)],
            Zs=dscr(f"Zs_{nm}", [S, 2048], BF16),
            Ys=dscr(f"Ys_{nm}", [128, 2, n2c, D], BF16),
            fT=dscr(f"fT_{nm}", [D, S], BF16),
            qT=dscr(f"qT_{nm}", [D, S], BF16), kT=dscr(f"kT_{nm}", [D, S], BF16),
            ktm=dscr(f"ktm_{nm}", [S, D], BF16), vtm=dscr(f"vtm_{nm}", [S, RV], BF16),
            of=dscr(f"of_{nm}", [S, RV], F32), rT=dscr(f"rT_{nm}", [RV, S], BF16),
            x1=dscr(f"x1_{nm}", [S, D], F32), x1T=dscr(f"x1T_{nm}", [D, S], BF16),
        )
        if debug:
            scr[nm].update(dmT=dscr(f"dmT_{nm}", [D, S], BF16), dcT=dscr(f"dcT_{nm}", [D, S], BF16),
                           dpre=dscr(f"dpre_{nm}", [S, D], F32), dm=dscr(f"dm_{nm}", [D, S], F32))
    scrR = {nm: {k: Res(f"{k}_{nm}") for k in scr[nm]} for nm, _ in seqs}
    for nm, _ in seqs:
        scrR[nm]["xT"] = [Res("xT0"), Res("xT1")]
        scrR[nm]["xres"] = [Res("xr0"), Res("xr1")]
    yR = {nm: Res("y" + nm) for nm, _ in seqs}
    inR = Res("inputs")

    with contextlib.ExitStack() as gst:
        P = Prog(nc, gst)
        psb = [gst.enter_context(nc.psum_tensor(f"psb{i}", [128, 512], F32)) for i in range(8)]
        psR = [Res(f"ps{i}") for i in range(8)]
        pstate = dict(i=0, ev=0)

        def take_ps():
            i = pstate["i"]
            pstate["i"] = (i + 1) % 8
            return psb[i], psR[i]

        def evq():
            pstate["ev"] ^= 1
            return "act" if pstate["ev"] else "dve"

        def copy_op(en, out, in_):
            if en == "act":
                return lambda h: h.activation(out=out, in_=in_, func=AF.Copy)
            return lambda h: h.tensor_copy(out=out, in_=in_)

        ident = T(nc, gst, "ident", [128, 128], BF16)
        maskT = T(nc, gst, "maskT", [128, 2, 128], F32)
        pcols = T(nc, gst, "pcols", [128, 4], F32)
        chan = T(nc, gst, "chan", [128, 2, 512], BF16)
        dec = T(nc, gst, "dec", [128, 24], F32)
        dtmp = T(nc, gst, "dtmp", [128, 24], F32)
        cw = T(nc, gst, "cw", [128, 8, 3], F32)
        epsc = T(nc, gst, "epsc", [128, 2], F32)
        P.dma("sp", ident.t[:], c_ident[:], reads=[inR], writes=[ident.R])
        P.dma("sp", maskT.t[:], c_mask.rearrange("d j i -> j d i"), reads=[inR], writes=[maskT.R])
        P.dma("sp", pcols.t[:], c_pcols[:], reads=[inR], writes=[pcols.R])
        P.dma("sp", chan.t[:], c_chan.rearrange("(k p) c -> p k c", p=128), reads=[inR], writes=[chan.R])
        P.op("dve", lambda h: h.memset(epsc.t[:, 0:1], LN_EPS), pwrites=[epsc.R])
        P.op("dve", lambda h: h.memset(epsc.t[:, 1:2], HN_EPS), pwrites=[epsc.R])

        for l in range(L):
            for k in ("in", "fi", "fout", "ro", "fo", "co", "o"):
                rows = wsrc[k].shape[1]
                for r0 in range(0, rows, 256):
                    r1 = min(rows, r0 + 256)
                    P.dma("bg", wb[k][l, r0:r1, :], wsrc[k][l, r0:r1, :], reads=[inR], pwrites=[wbR[k][l]])

        def load_w(wt, key, l, r0, nk, c0, ncols):
            src = wb[key][l, r0:r0 + nk * 128, c0:c0 + ncols].rearrange("(k p) c -> p k c", p=128)
            P.dma("sp", wt.t[:, 0:nk, 0:ncols], src, reads=[wbR[key][l]], writes=[wt.R])

        class WStream:
            def __init__(self, tiles, reqs, hold=1):
                self.tiles, self.reqs, self.issued, self.i, self.hold = tiles, reqs, 0, 0, hold

            def next(self, spec):
                n = len(self.tiles)
                assert self.reqs[self.i] == spec, (self.i, self.reqs[self.i], spec)
                while self.issued < len(self.reqs) and self.issued <= self.i + n - self.hold:
                    key, l_, r0, nk, c0, ncols = self.reqs[self.issued]
                    load_w(self.tiles[self.issued % n], key, l_, r0, nk, c0, ncols)
                    self.issued += 1
                w = self.tiles[self.i % n]
                self.i += 1
                return w

        def transposes(src_fn, srcR, nblk, dst_fn, dstR):
            for g in range(nblk // 4):
                bank, bR = take_ps()
                P.mm([(lambda h, j=j: h.matmul(bank[:, j * 128:(j + 1) * 128], lhsT=src_fn(g * 4 + j),
                                                rhs=ident.t[:], start=True, stop=True)) for j in range(4)],
                     reads=[srcR, ident.R], writes=[bR])
                en = evq()
                P.op(en, copy_op(en, dst_fn(g), bank[:].rearrange("p (j c) -> p j c", j=4)),
                     reads=[bR], pwrites=[dstR])

        def layer_consts(l):
            P.dma("sp", dtmp.t[:, 0:8], decay_in[l:l + 1, :].partition_broadcast(128), reads=[inR], writes=[dtmp.R])
            P.op("act", lambda h: h.activation(out=dtmp.t[:, 8:16], in_=dtmp.t[:, 0:8], func=AF.Exp, scale=-1.0),
                 reads=[dtmp.R], writes=[dtmp.R])
            P.op("dve", lambda h: h.tensor_scalar(out=dtmp.t[:, 8:16], in0=dtmp.t[:, 8:16], scalar1=1.0, scalar2=None,
                                                  op0=ALU.add), reads=[dtmp.R], writes=[dtmp.R])
            P.op("act", lambda h: h.activation(out=dtmp.t[:, 16:24], in_=dtmp.t[:, 8:16], func=AF.Ln),
                 reads=[dtmp.R], writes=[dtmp.R])
            lnv = dtmp.t[:, 16:24]
            for d in range(2):
                sl = slice(16 + 4 * d, 20 + 4 * d)
                rs_col = 1 if d == 0 else 3
                cs_col = 0 if d == 0 else 2
                P.op("act", lambda h, d=d, sl=sl, c=rs_col: h.activation(
                    out=dec.t[:, 4 * d:4 * d + 4], in_=dtmp.t[:, sl], func=AF.Exp, scale=pcols.t[:, c:c + 1]),
                    reads=[dtmp.R, pcols.R], writes=[dec.R])
                P.op("act", lambda h, d=d, sl=sl, c=cs_col: h.activation(
                    out=dec.t[:, 8 + 4 * d:12 + 4 * d], in_=dtmp.t[:, sl], func=AF.Exp, scale=pcols.t[:, c:c + 1]),
                    reads=[dtmp.R, pcols.R], writes=[dec.R])
                P.op("act", lambda h, d=d, sl=sl: h.activation(
                    out=dec.t[:, 16 + 4 * d:20 + 4 * d], in_=dtmp.t[:, sl], func=AF.Exp, scale=-128.0),
                    reads=[dtmp.R], writes=[dec.R])
            P.op("dve", lambda h: h.tensor_scalar(out=dec.t[:, 8:16], in0=dec.t[:, 8:16], scalar1=1.0 / 16.0,
                                                  scalar2=None, op0=ALU.mult), reads=[dec.R], writes=[dec.R])
            for j in range(3):
                P.dma("sp", cw.t[:, :, j:j + 1], convw_in[l, j:j + 1, :].rearrange("o (b p) -> p b o", p=128),
                      reads=[inR], pwrites=[cw.R], allow_slow_non_contiguous=True)

        def load_lnp(st, l, which):
            lnp = T(nc, st, "lnp", [128, 2, D], F32)
            P.dma("sp", lnp.t[:, 0, :], lng_in[l, which:which + 1, :].partition_broadcast(128), reads=[inR], pwrites=[lnp.R])
            P.dma("sp", lnp.t[:, 1, :], lnb_in[l, which:which + 1, :].partition_broadcast(128), reads=[inR], pwrites=[lnp.R])
            return lnp

        def ln_math(lnp, pre, dst_rows, dstR_f, xb4, stt, mv, rs):
            g_ap = lnp.t[:, 0, :]
            b_ap = lnp.t[:, 1, :]

            def ln_chunk(c):
                pr = pre.Rs[c]
                xc = pre.t[:, c, :]
                for hh in range(2):
                    P.op("dve", lambda h, hh=hh, xc=xc: h.bn_stats(out=stt.t[:, hh, :], in_=xc[:, hh * 512:(hh + 1) * 512]),
                         reads=[pr], pwrites=[stt.R])
                P.op("dve", lambda h: h.bn_aggr(out=mv.t[:], in_=stt.t[:, 0:2, :]), reads=[stt.R], writes=[mv.R])
                P.op("act", lambda h: h.activation(out=rs.t[:, 0:1], in_=mv.t[:, 1:2], func=AF.Sqrt, bias=epsc.t[:, 0:1]),
                     reads=[mv.R, epsc.R], writes=[rs.R])
                P.op("dve", lambda h: h.reciprocal(out=rs.t[:, 1:2], in_=rs.t[:, 0:1]), reads=[rs.R], writes=[rs.R])
                P.op("dve", lambda h, xc=xc: h.scalar_tensor_tensor(out=xc, in0=xc, scalar=mv.t[:, 0:1], in1=g_ap,
                                                                    op0=ALU.subtract, op1=ALU.mult),
                     reads=[pr, mv.R, lnp.R], writes=[pr])
                P.op("dve", lambda h, xc=xc: h.scalar_tensor_tensor(out=xc, in0=xc, scalar=rs.t[:, 1:2], in1=b_ap,
                                                                    op0=ALU.mult, op1=ALU.add),
                     reads=[pr, rs.R, lnp.R], writes=[pr])
                P.op("act", lambda h, xc=xc, c=c: h.activation(out=xb4.t[:, c, :], in_=xc, func=AF.Copy),
                     reads=[pr], writes=[xb4.Rs[c]])

            ln_fns = [(lambda c=c: ln_chunk(c)) for c in range(4)]
            st_fns = [(lambda c=c: P.dma("sp", dst_rows[c * 128:(c + 1) * 128, :], pre.t[:, c, :], reads=[pre.Rs[c]],
                                         pwrites=[dstR_f])) for c in range(4)]
            return ln_fns, st_fns

        def ln_T(xb4, xTt):
            for c in range(4):
                transposes(lambda blk, c=c: xb4.t[:, c, blk * 128:(blk + 1) * 128], xb4.Rs[c], 8,
                           lambda g, c=c: xTt.t[:, g * 4:(g + 1) * 4, c * 128:(c + 1) * 128], xTt.R)

        def phase_x0(nm, S):
            with contextlib.ExitStack() as st:
                xc = [T(nc, st, f"x0c{i}", [128, D], F32) for i in range(2)]
                xb = [T(nc, st, f"x0b{i}", [128, D], BF16) for i in range(2)]
                xTt = [T(nc, st, f"x0T{i}", [128, 8, TT], BF16) for i in range(2)]
                for t in range(S // TT):
                    xt = xTt[t % 2]
                    for c in range(4):
                        i = (t * 4 + c) % 2
                        r0 = t * TT + c * 128
                        P.dma("sp", xc[i].t[:], xin[nm][r0:r0 + 128, :], reads=[inR], writes=[xc[i].R])
                        P.op("act", lambda h, i=i: h.activation(out=xb[i].t[:], in_=xc[i].t[:], func=AF.Copy),
                             reads=[xc[i].R], writes=[xb[i].R])
                        transposes(lambda blk, i=i: xb[i].t[:, blk * 128:(blk + 1) * 128], xb[i].R, 8,
                                   lambda g, c=c: xt.t[:, g * 4:(g + 1) * 4, c * 128:(c + 1) * 128], xt.R)
                    P.dma("sp", scr[nm]["xT"][0][:, t * TT:(t + 1) * TT].rearrange("(k p) t -> p k t", p=128),
                          xt.t[:], reads=[xt.R], pwrites=[scrR[nm]["xT"][0]])
                P.barrier(label="X0_" + nm)

        def ret_prep(d, ci, qTt, kTt, v_fn, vR, vts, PTs):
            cs = slice(ci * 128, (ci + 1) * 128)
            for h in range(4):
                idx = d * 4 + h
                P.op("act", lambda hh, h=h, idx=idx: hh.activation(out=vts[h].t[:], in_=v_fn(h), func=AF.Copy,
                                                                    scale=dec.t[:, 8 + idx:9 + idx]),
                     reads=[vR, dec.R], writes=[vts[h].R])
            bankS, bSR = take_ps()
            P.mm([(lambda hh, h=h, kk=kk: hh.matmul(bankS[:, h * 128:(h + 1) * 128], lhsT=kTt.t[:, 2 * h + kk, cs],
                                                    rhs=qTt.t[:, 2 * h + kk, cs], start=(kk == 0), stop=(kk == 1)))
                  for h in range(4) for kk in range(2)], reads=[kTt.R, qTt.R], writes=[bSR])
            for h in range(4):
                P.op("dve", lambda hh, h=h: hh.tensor_tensor(out=PTs[h].t[:], in0=bankS[:, h * 128:(h + 1) * 128],
                                                             in1=maskT.t[:, d, :], op=ALU.mult),
                     reads=[bSR, maskT.R], writes=[PTs[h].R])

        def ret_main(d, ci, qTt, kTt, ktm_ap, ktmR, vts, PTs, U, Sbf, o_fn, oR, accumulate, mid=None):
            cs = slice(ci * 128, (ci + 1) * 128)
            for hp in range(2):
                heads = (2 * hp, 2 * hp + 1)
                kvb = {}
                for h in heads:
                    for kk in range(2):
                        bank3, b3R = take_ps()
                        P.mm([lambda hh, kk=kk, h=h, bank3=bank3: hh.matmul(
                            bank3[:], lhsT=ktm_ap[:, h * 256 + kk * 128:h * 256 + (kk + 1) * 128], rhs=vts[h].t[:],
                            start=True, stop=True)], reads=[ktmR, vts[h].R], writes=[b3R])
                        kvb[(h, kk)] = (bank3, b3R)
                ob = {}
                for h in heads:
                    bank2, b2R = take_ps()
                    P.mm([lambda hh, h=h, bank2=bank2: hh.matmul(bank2[:], lhsT=PTs[h].t[:], rhs=vts[h].t[:], start=True, stop=False),
                          lambda hh, h=h, bank2=bank2: hh.matmul(bank2[:], lhsT=qTt.t[:, 2 * h, cs], rhs=Sbf.t[:, h, 0, :],
                                                                 start=False, stop=False),
                          lambda hh, h=h, bank2=bank2: hh.matmul(bank2[:], lhsT=qTt.t[:, 2 * h + 1, cs], rhs=Sbf.t[:, h, 1, :],
                                                                 start=False, stop=True)],
                         reads=[PTs[h].R, vts[h].R, qTt.R, Sbf.Rs[2 * h], Sbf.Rs[2 * h + 1]], writes=[b2R])
                    ob[h] = (bank2, b2R)
                if hp == 1 and mid is not None:
                    mid()
                for h in heads:
                    idx = d * 4 + h
                    bank2, b2R = ob[h]
                    o_ap = o_fn(h)
                    if accumulate:
                        P.op("dve", lambda hh, o_ap=o_ap, bank2=bank2, idx=idx: hh.scalar_tensor_tensor(
                            out=o_ap, in0=bank2[:], scalar=dec.t[:, idx:idx + 1], in1=o_ap, op0=ALU.mult, op1=ALU.add),
                            reads=[b2R, dec.R, oR], writes=[oR])
                    else:
                        P.op("act", lambda hh, o_ap=o_ap, bank2=bank2, idx=idx: hh.activation(
                            out=o_ap, in_=bank2[:], func=AF.Copy, scale=dec.t[:, idx:idx + 1]),
                            reads=[b2R, dec.R], pwrites=[oR])
                    for kk in range(2):
                        bank3, b3R = kvb[(h, kk)]
                        uR = U.Rs[2 * h + kk]
                        P.op("dve", lambda hh, kk=kk, h=h, bank3=bank3, idx=idx: hh.scalar_tensor_tensor(
                            out=U.t[:, h, kk, :], in0=U.t[:, h, kk, :], scalar=dec.t[:, 16 + idx:17 + idx], in1=bank3[:],
                            op0=ALU.mult, op1=ALU.add), reads=[b3R, dec.R, uR], writes=[uR])
                        P.op("act", lambda hh, kk=kk, h=h, idx=idx: hh.activation(
                            out=Sbf.t[:, h, kk, :], in_=U.t[:, h, kk, :], func=AF.Copy, scale=dec.t[:, 16 + idx:17 + idx]),
                            reads=[uR, dec.R], writes=[Sbf.Rs[2 * h + kk]])

        def init_state(U, Sbf):
            for i in range(8):
                P.op("dve", lambda hh, i=i: hh.memset(U.t[:, i // 2, i % 2, :], 0.0), writes=[U.Rs[i]])
                P.op("pool", lambda hh, i=i: hh.memset(Sbf.t[:, i // 2, i % 2, :], 0.0), writes=[Sbf.Rs[i]])

        def phase_a(nm, S, l, par):
            sc, sR = scr[nm], scrR[nm]
            with contextlib.ExitStack() as st:
                xTt = [T(nc, st, f"a_xT{i}", [128, 8, TT], BF16) for i in range(2)]
                W = [T(nc, st, f"a_w{i}", [128, 8, 512], BF16) for i in range(4)]
                cs_t = [T(nc, st, f"a_cs{i}", [128, 2, TT], F32) for i in range(2)]
                uT = T(nc, st, "a_uT", [128, 8, TT], BF16)
                qTt = T(nc, st, "a_qT", [128, 8, TT], BF16)
                kTt = T(nc, st, "a_kT", [128, 8, TT], BF16)
                Zt = [T(nc, st, f"a_Z{i}", [128, 2048], BF16) for i in range(2)]
                ktm = [T(nc, st, f"a_ktm{i}", [128, D], BF16) for i in range(4)]
                vtm = T(nc, st, "a_vtm", [128, 4, RV], BF16, nres=4)
                vt = [T(nc, st, f"a_vt{i}", [128, DV], BF16) for i in range(8)]
                PT = [T(nc, st, f"a_PT{i}", [128, 128], BF16) for i in range(8)]
                U = T(nc, st, "a_U", [128, 4, 2, DV], F32, nres=8)
                Sbf = T(nc, st, "a_Sbf", [128, 4, 2, DV], BF16, nres=8)
                ot = [T(nc, st, f"a_o{i}", [128, RV], F32) for i in range(2)]
                rt = [T(nc, st, f"a_rt{i}", [128, TT], F32) for i in range(8)]
                ntile = S // TT
                per_tile = ([("in", l, 0, 8, C_F + cg * 512, 512) for cg in range(2)]
                            + [("in", l, 0, 8, C_Q + cg * 512, 512) for cg in range(2)]
                            + [("in", l, 0, 8, C_K + cg * 512, 512) for cg in range(2)]
                            + [("in", l, 0, 8, C_V + cg * 512, 512) for cg in range(4)])
                ws = WStream(W, per_tile * ntile)

                def loads(t):
                    xt = xTt[t % 2]
                    cst = cs_t[t % 2]
                    a0 = t * TT
                    P.dma("sp", xt.t[:], sc["xT"][par][:, a0:a0 + TT].rearrange("(k p) t -> p k t", p=128),
                          reads=[sR["xT"][par]], writes=[xt.R])
                    P.dma("sp", cst.t[:, 0, :], c_cos[:, a0:a0 + TT], reads=[inR], writes=[cst.R])
                    P.dma("sp", cst.t[:, 1, :], c_sin[:, a0:a0 + TT], reads=[inR], pwrites=[cst.R])

                init_state(U, Sbf)
                loads(0)
                for t in range(ntile):
                    xt = xTt[t % 2]
                    cst = cs_t[t % 2]
                    a0 = t * TT
                    if t + 1 < ntile:
                        loads(t + 1)
                    for cg in range(2):
                        w = ws.next(("in", l, 0, 8, C_F + cg * 512, 512))
                        for rb in range(4):
                            bank, bR = take_ps()
                            P.mm([(lambda h, k=k, rb=rb, w=w: h.matmul(bank[:], lhsT=w.t[:, k, rb * 128:(rb + 1) * 128],
                                                                       rhs=xt.t[:, k, :], start=(k == 0), stop=(k == 7)))
                                  for k in range(8)], reads=[w.R, xt.R], writes=[bR])
                            en = evq()
                            P.op(en, copy_op(en, uT.t[:, cg * 4 + rb, :], bank[:]), reads=[bR], pwrites=[uT.R])
                    for qk, (c0, dstT) in enumerate(((C_Q, qTt), (C_K, kTt))):
                        for cg in range(2):
                            w = ws.next(("in", l, 0, 8, c0 + cg * 512, 512))
                            for hh2 in range(2):
                                head = cg * 2 + hh2
                                banks = []
                                for half in range(2):
                                    rb = hh2 * 2 + half
                                    bank, bR = take_ps()
                                    P.mm([(lambda h, k=k, rb=rb, w=w, bank=bank: h.matmul(
                                        bank[:], lhsT=w.t[:, k, rb * 128:(rb + 1) * 128], rhs=xt.t[:, k, :],
                                        start=(k == 0), stop=(k == 7))) for k in range(8)],
                                        reads=[w.R, xt.R], writes=[bR])
                                    banks.append((bank, bR))
                                (b1, b1R), (b2, b2R) = banks
                                ra, rb_, rc, rd = (rt[(qk * 4 + j) % 8] for j in range(4))
                                cosap, sinap = cst.t[:, 0, :], cst.t[:, 1, :]
                                P.op("dve", lambda h, b1=b1, ra=ra: h.tensor_tensor(out=ra.t[:], in0=b1[:], in1=cosap, op=ALU.mult),
                                     reads=[b1R, cst.R], writes=[ra.R])
                                P.op("dve", lambda h, b2=b2, rb_=rb_: h.tensor_tensor(out=rb_.t[:], in0=b2[:], in1=sinap, op=ALU.mult),
                                     reads=[b2R, cst.R], writes=[rb_.R])
                                P.op("dve", lambda h, b1=b1, rc=rc: h.tensor_tensor(out=rc.t[:], in0=b1[:], in1=sinap, op=ALU.mult),
                                     reads=[b1R, cst.R], writes=[rc.R])
                                P.op("dve", lambda h, b2=b2, rd=rd: h.tensor_tensor(out=rd.t[:], in0=b2[:], in1=cosap, op=ALU.mult),
                                     reads=[b2R, cst.R], writes=[rd.R])
                                P.op("pool", lambda h, ra=ra, rb_=rb_, head=head, dstT=dstT: h.tensor_tensor(
                                    out=dstT.t[:, 2 * head, :], in0=ra.t[:], in1=rb_.t[:], op=ALU.subtract),
                                    reads=[ra.R, rb_.R], pwrites=[dstT.R])
                                P.op("pool", lambda h, rc=rc, rd=rd, head=head, dstT=dstT: h.tensor_tensor(
                                    out=dstT.t[:, 2 * head + 1, :], in0=rc.t[:], in1=rd.t[:], op=ALU.add),
                                    reads=[rc.R, rd.R], pwrites=[dstT.R])
                    late = []
                    late.append(lambda a0=a0: P.dma("sp", sc["qT"][:, a0:a0 + TT].rearrange("(k p) t -> p k t", p=128), qTt.t[:],
                                                    reads=[qTt.R], pwrites=[sR["qT"]]))
                    late.append(lambda a0=a0: P.dma("sp", sc["kT"][:, a0:a0 + TT].rearrange("(k p) t -> p k t", p=128), kTt.t[:],
                                                    reads=[kTt.R], pwrites=[sR["kT"]]))
                    for c in range(4):
                        zt = Zt[c % 2]
                        for g in range(4):
                            bank, bR = take_ps()
                            P.mm([(lambda h, kk=kk, g=g, c=c: h.matmul(bank[:], lhsT=uT.t[:, 2 * g + kk, c * 128:(c + 1) * 128],
                                                                       rhs=chan.t[:, kk, :], start=(kk == 0), stop=(kk == 1)))
                                  for kk in range(2)], reads=[uT.R, chan.R], writes=[bR])
                            en = evq()
                            dst = zt.t[:].rearrange("p (r c) -> p r c", r=2)[:, :, g * 256:(g + 1) * 256]
                            P.op(en, copy_op(en, dst, bank[:].rearrange("p (r c) -> p r c", r=2)),
                                 reads=[bR], writes=[zt.R] if g == 0 else (), pwrites=() if g == 0 else [zt.R])
                        r0 = a0 + c * 128
                        zst = (lambda r0=r0, zt=zt: P.dma("sp", sc["Zs"][r0:r0 + 128, :], zt.t[:], reads=[zt.R], pwrites=[sR["Zs"]]))
                        if c >= 1:
                            zprev()
                        zprev = zst
                    zprev()
                    for cg in range(4):
                        w = ws.next(("in", l, 0, 8, C_V + cg * 512, 512))
                        for c in range(4):
                            bank, bR = take_ps()
                            P.mm([(lambda h, k=k, c=c, w=w: h.matmul(bank[:], lhsT=xt.t[:, k, c * 128:(c + 1) * 128],
                                                                     rhs=w.t[:, k, :], start=(k == 0), stop=(k == 7)))
                                  for k in range(8)], reads=[w.R, xt.R], writes=[bR])
                            en = evq()
                            P.op(en, copy_op(en, vtm.t[:, c, cg * 512:(cg + 1) * 512], bank[:]),
                                 reads=[bR], writes=[vtm.Rs[c]] if cg == 0 else (), pwrites=() if cg == 0 else [vtm.Rs[c]])
                    for c in range(4):
                        r0 = a0 + c * 128
                        km = ktm[c]
                        transposes(lambda blk, c=c: kTt.t[:, blk, c * 128:(c + 1) * 128], kTt.R, 8,
                                   lambda g, km=km: km.t[:, g * 512:(g + 1) * 512].rearrange("p (j c) -> p j c", j=4), km.R)
                        late.append(lambda r0=r0, km=km: P.dma("sp", sc["ktm"][r0:r0 + 128, :], km.t[:], reads=[km.R], pwrites=[sR["ktm"]]))
                        late.append(lambda r0=r0, c=c: P.dma("sp", sc["vtm"][r0:r0 + 128, :], vtm.t[:, c, :], reads=[vtm.Rs[c]],
                                                            pwrites=[sR["vtm"]]))
                    oprev = None
                    for c in range(4):
                        r0 = a0 + c * 128
                        km = ktm[c]
                        o = ot[c % 2]
                        st4 = (c % 2) * 4
                        def prep_a(c):
                            s4 = (c % 2) * 4
                            ret_prep(0, c, qTt, kTt, lambda h_, c=c: vtm.t[:, c, h_ * DV:(h_ + 1) * DV], vtm.Rs[c],
                                     vt[s4:s4 + 4], PT[s4:s4 + 4])
                        if c == 0:
                            prep_a(0)
                        ret_main(0, c, qTt, kTt, km.t, km.R, vt[st4:st4 + 4], PT[st4:st4 + 4], U, Sbf,
                                 lambda h_, o=o: o.t[:, h_ * DV:(h_ + 1) * DV], o.R, False,
                                 mid=(lambda c=c: prep_a(c + 1)) if c < 3 else None)
                        if oprev is not None:
                            oprev()
                        oprev = (lambda r0=r0, o=o: P.dma("sp", sc["of"][r0:r0 + 128, :], o.t[:], reads=[o.R], pwrites=[sR["of"]]))
                        if c == 1:
                            for fn in late:
                                fn()
                    oprev()
                P.barrier(label="A_" + nm)

        def phase_f1(nm, S):
            sc, sR = scr[nm], scrR[nm]
            n2c = S // 128
            nb = min(4, n2c)
            zview = sc["Zs"].rearrange("(a b) c -> a b c", b=n2c)
            with contextlib.ExitStack() as st:
                Zi = [T(nc, st, f"f1_z{i}", [128, nb, 2048], BF16) for i in range(2)]
                Mi = [T(nc, st, f"f1_m{i}", [128, nb, 3, 128], BF16) for i in range(2)]
                Yo = [T(nc, st, f"f1_y{i}", [128, nb, 2048], BF16, nres=nb) for i in range(2)]
                def loads(gi):
                    n20 = gi * nb
                    z, m = Zi[gi % 2], Mi[gi % 2]
                    P.dma("sp", z.t[:], zview[:, n20:n20 + nb, :], reads=[sR["Zs"]], writes=[z.R])
                    P.dma("sp", m.t[:], c_t1[S][n20:n20 + nb].rearrange("n p a b -> p n a b"), reads=[inR], writes=[m.R])

                loads(0)
                for gi, n20 in enumerate(range(0, n2c, nb)):
                    z, m, y = Zi[gi % 2], Mi[gi % 2], Yo[gi % 2]
                    if n20 + nb < n2c:
                        loads(gi + 1)
                    for j in range(nb):
                        for ri in range(2):
                            for cb in range(2):
                                bank, bR = take_ps()
                                zr = z.t[:, j, cb * 512:(cb + 1) * 512]
                                zi = z.t[:, j, 1024 + cb * 512:1024 + (cb + 1) * 512]
                                if ri == 0:
                                    ops = [(0, zr), (1, zi)]
                                else:
                                    ops = [(0, zi), (2, zr)]
                                P.mm([(lambda h, a=a, rhs=rhs, first=(ii == 0), bank=bank: h.matmul(
                                    bank[:], lhsT=m.t[:, j, a, :], rhs=rhs, start=first, stop=not first))
                                    for ii, (a, rhs) in enumerate(ops)], reads=[z.R, m.R], writes=[bR])
                                en = evq()
                                first = (ri == 0 and cb == 0)
                                P.op(en, copy_op(en, y.t[:, j, ri * 1024 + cb * 512:ri * 1024 + (cb + 1) * 512], bank[:]),
                                     reads=[bR], writes=[y.Rs[j]] if first else (), pwrites=() if first else [y.Rs[j]])
                    if gi > 0:
                        yprev()

                    def yst(y=y, n20=n20):
                        for ri in range(2):
                            P.dma("sp", sc["Ys"][:, ri, n20:n20 + nb, :], y.t[:, :, ri * 1024:(ri + 1) * 1024],
                                  reads=y.Rs, pwrites=[sR["Ys"]])
                    yprev = yst
                yprev()
                P.barrier(label="F1_" + nm)

        def phase_f3(nm, S):
            sc, sR = scr[nm], scrR[nm]
            n2c = S // 128
            K2 = 2 * n2c
            G = min(128, 512 // n2c)
            with contextlib.ExitStack() as st:
                Yi = [T(nc, st, f"f3_y{i}", [128, 128, 128], BF16) for i in range(2)]
                t3 = T(nc, st, "f3_t3", [128, n2c], BF16)
                fo = [T(nc, st, f"f3_o{i}", [128, S], BF16) for i in range(2)]
                P.dma("sp", t3.t[0:K2, :], c_t3[S][:, :], reads=[inR], writes=[t3.R])
                def loads(cc):
                    y = Yi[cc % 2]
                    for ri in range(2):
                        P.dma("sp", y.t[ri * n2c:(ri + 1) * n2c, :, :],
                              sc["Ys"][:, ri, :, cc * 128:(cc + 1) * 128].rearrange("k n c -> n k c"),
                              reads=[sR["Ys"]], writes=[y.R] if ri == 0 else (), pwrites=() if ri == 0 else [y.R])

                loads(0)
                for cc in range(8):
                    y, f = Yi[cc % 2], fo[cc % 2]
                    if cc + 1 < 8:
                        loads(cc + 1)
                    fv = f.t[:].rearrange("p (b a) -> p a b", a=128)
                    for g0 in range(0, 128, G):
                        bank, bR = take_ps()
                        P.mm([(lambda h, k1=k1, bank=bank: h.matmul(bank[:, (k1 - g0) * n2c:(k1 - g0 + 1) * n2c],
                                                                    lhsT=y.t[0:K2, k1, :], rhs=t3.t[0:K2, :],
                                                                    start=True, stop=True)) for k1 in range(g0, g0 + G)],
                             reads=[y.R, t3.R], writes=[bR])
                        en = evq()
                        P.op(en, copy_op(en, fv[:, g0:g0 + G, :], bank[:, 0:G * n2c].rearrange("p (a b) -> p a b", b=n2c)),
                             reads=[bR], pwrites=[f.R])
                    if cc > 0:
                        fprev()
                    fprev = (lambda cc=cc, f=f: P.dma("sp", sc["fT"][cc * 128:(cc + 1) * 128, :], f.t[:], reads=[f.R], pwrites=[sR["fT"]]))
                fprev()
                P.barrier(label="F3_" + nm)

        def phase_b1(nm, S, l, par):
            sc, sR = scr[nm], scrR[nm]
            with contextlib.ExitStack() as st:
                xTt = [T(nc, st, f"b_xT{i}", [128, 8, TT], BF16) for i in range(2)]
                qTt = [T(nc, st, f"b_qT{i}", [128, 8, TT], BF16) for i in range(2)]
                kTt = [T(nc, st, f"b_kT{i}", [128, 8, TT], BF16) for i in range(2)]
                W = [T(nc, st, f"b_w{i}", [128, 8, 512], BF16) for i in range(2)]
                sg = T(nc, st, "b_sg", [128, 4, RV], BF16, nres=4)
                ktm = [T(nc, st, f"b_ktm{i}", [128, D], BF16) for i in range(3)]
                vtm = [T(nc, st, f"b_vtm{i}", [128, RV], BF16) for i in range(3)]
                ot = [T(nc, st, f"b_o{i}", [128, RV], F32) for i in range(3)]
                vt = [T(nc, st, f"b_vt{i}", [128, DV], BF16) for i in range(8)]
                PT = [T(nc, st, f"b_PT{i}", [128, 128], BF16) for i in range(8)]
                U = T(nc, st, "b_U", [128, 4, 2, DV], F32, nres=8)
                Sbf = T(nc, st, "b_Sbf", [128, 4, 2, DV], BF16, nres=8)
                rstages = [T(nc, st, f"b_rst{i}", [128, RV], BF16) for i in range(2)]
                rTts = [T(nc, st, f"b_rT{i}", [128, 16, TT], BF16) for i in range(2)]
                stt = T(nc, st, "b_stt", [128, 4, 6], F32)
                mv = T(nc, st, "b_mv", [128, 4, 2], F32)
                rs = T(nc, st, "b_rs", [128, 12], F32)
                init_state(U, Sbf)
                ntile = S // TT
                ws = WStream(W, [("in", l, 0, 8, C_G + cg * 512, 512) for cg in range(4)] * ntile)
                cidx = 0
                pending = [None]
                pending_prep = [None]

                def loads(it):
                    t = ntile - 1 - it
                    a0 = t * TT
                    for (tile_, key, resv) in ((xTt[it % 2], "xT", sR["xT"][par]), (qTt[it % 2], "qT", sR["qT"]),
                                               (kTt[it % 2], "kT", sR["kT"])):
                        src = sc["xT"][par] if key == "xT" else sc[key]
                        P.dma("sp", tile_.t[:], src[:, a0:a0 + TT].rearrange("(k p) t -> p k t", p=128),
                              reads=[resv], writes=[tile_.R])

                chunk_list = [(t_, c_) for t_ in reversed(range(ntile)) for c_ in reversed(range(4))]

                def chunk_loads(i):
                    t_, c_ = chunk_list[i]
                    r0_ = t_ * TT + c_ * 128
                    km_, vm_, o_ = ktm[i % 3], vtm[i % 3], ot[i % 3]
                    P.dma("sp", km_.t[:], sc["ktm"][r0_:r0_ + 128, :], reads=[sR["ktm"]], writes=[km_.R])
                    P.dma("sp", vm_.t[:], sc["vtm"][r0_:r0_ + 128, :], reads=[sR["vtm"]], writes=[vm_.R])
                    P.dma("sp", o_.t[:], sc["of"][r0_:r0_ + 128, :], reads=[sR["of"]], writes=[o_.R])

                loads(0)
                chunk_loads(0)
                for t in reversed(range(ntile)):
                    it = ntile - 1 - t
                    xt, qt, kt = xTt[it % 2], qTt[it % 2], kTt[it % 2]
                    a0 = t * TT
                    if it + 1 < ntile:
                        loads(it + 1)
                    for cg in range(4):
                        w = ws.next(("in", l, 0, 8, C_G + cg * 512, 512))
                        for c in range(4):
                            bank, bR = take_ps()
                            P.mm([(lambda h, k=k, c=c, w=w: h.matmul(bank[:], lhsT=xt.t[:, k, c * 128:(c + 1) * 128],
                                                                     rhs=w.t[:, k, :], start=(k == 0), stop=(k == 7)))
                                  for k in range(8)], reads=[w.R, xt.R], writes=[bR])
                            P.op("act", lambda h, c=c, cg=cg: h.activation(out=sg.t[:, c, cg * 512:(cg + 1) * 512], in_=bank[:],
                                                                           func=AF.Silu),
                                 reads=[bR], writes=[sg.Rs[c]] if cg == 0 else (), pwrites=() if cg == 0 else [sg.Rs[c]])
                    rTt = rTts[it % 2]
                    if pending_prep[0] is not None:
                        prep_b(pending_prep[0])
                        pending_prep[0] = None
                    for c in reversed(range(4)):
                        r0 = a0 + c * 128
                        km, vm, o = ktm[cidx % 3], vtm[cidx % 3], ot[cidx % 3]
                        rstage = rstages[cidx % 2]
                        st4 = (cidx % 2) * 4
                        cidx += 1
                        if cidx < len(chunk_list):
                            chunk_loads(cidx)
                        def prep_b(i):
                            t_, c_ = chunk_list[i]
                            it_ = ntile - 1 - t_
                            vm_ = vtm[i % 3]
                            s4 = (i % 2) * 4
                            ret_prep(1, c_, qTt[it_ % 2], kTt[it_ % 2], lambda h_, vm_=vm_: vm_.t[:, h_ * DV:(h_ + 1) * DV], vm_.R,
                                     vt[s4:s4 + 4], PT[s4:s4 + 4])
                        ci_ = cidx - 1
                        if ci_ == 0:
                            prep_b(0)
                        nxt_same_tile = (c > 0)
                        ret_main(1, c, qt, kt, km.t, km.R, vt[st4:st4 + 4], PT[st4:st4 + 4], U, Sbf,
                                 lambda h_, o=o: o.t[:, h_ * DV:(h_ + 1) * DV], o.R, True,
                                 mid=(lambda ci_=ci_: prep_b(ci_ + 1)) if nxt_same_tile else None)
                        if (not nxt_same_tile) and ci_ + 1 < len(chunk_list):
                            pending_prep[0] = ci_ + 1
                        if pending[0] is not None:
                            pending[0]()
                            pending[0] = None
                        for h_ in range(4):
                            P.op("dve", lambda h, h_=h_, o=o: h.bn_stats(out=stt.t[:, h_, :], in_=o.t[:, h_ * DV:(h_ + 1) * DV]),
                                 reads=[o.R], pwrites=[stt.R])
                        for h_ in range(4):
                            P.op("dve", lambda h, h_=h_: h.bn_aggr(out=mv.t[:, h_, :], in_=stt.t[:, h_, :]),
                                 reads=[stt.R], pwrites=[mv.R])
                        P.op("act", lambda h: h.activation(out=rs.t[:, 0:4], in_=mv.t[:, :, 1], func=AF.Sqrt, bias=epsc.t[:, 1:2]),
                             reads=[mv.R, epsc.R], writes=[rs.R])
                        P.op("dve", lambda h: h.reciprocal(out=rs.t[:, 4:8], in_=rs.t[:, 0:4]), reads=[rs.R], writes=[rs.R])
                        P.op("dve", lambda h: h.scalar_tensor_tensor(out=rs.t[:, 8:12], in0=mv.t[:, :, 0], scalar=-1.0, in1=rs.t[:, 4:8],
                                                                     op0=ALU.mult, op1=ALU.mult), reads=[mv.R, rs.R], writes=[rs.R])
                        for h_ in range(4):
                            osl = o.t[:, h_ * DV:(h_ + 1) * DV]
                            P.op("act", lambda h, h_=h_, osl=osl: h.activation(out=osl, in_=osl, func=AF.Identity,
                                                                               scale=rs.t[:, 4 + h_:5 + h_], bias=rs.t[:, 8 + h_:9 + h_]),
                                 reads=[o.R, rs.R], writes=[o.R])
                        P.op("pool", lambda h, o=o, c=c, rstage=rstage: h.tensor_tensor(out=rstage.t[:, 0:1024], in0=o.t[:, 0:1024],
                                                                                        in1=sg.t[:, c, 0:1024], op=ALU.mult),
                             reads=[o.R, sg.Rs[c]], writes=[rstage.R])
                        P.op("pool", lambda h, o=o, c=c, rstage=rstage: h.tensor_tensor(out=rstage.t[:, 1024:2048], in0=o.t[:, 1024:2048],
                                                                                        in1=sg.t[:, c, 1024:2048], op=ALU.mult),
                             reads=[o.R, sg.Rs[c]], pwrites=[rstage.R])

                        def post_b(c=c, rstage=rstage, rTt=rTt, a0=a0):
                            transposes(lambda blk: rstage.t[:, blk * 128:(blk + 1) * 128], rstage.R, 16,
                                       lambda g: rTt.t[:, g * 4:(g + 1) * 4, c * 128:(c + 1) * 128], rTt.R)
                            if c == 0:
                                P.dma("sp", sc["rT"][:, a0:a0 + TT].rearrange("(k p) t -> p k t", p=128), rTt.t[:],
                                      reads=[rTt.R], pwrites=[sR["rT"]])
                        pending[0] = post_b
                if pending[0] is not None:
                    pending[0]()
                P.barrier(label="B1_" + nm)

        def phase_b2(nm, S, l, par):
            sc, sR = scr[nm], scrR[nm]
            xres_src = xin[nm] if l == 0 else sc["xres"][par]
            xres_R = inR if l == 0 else sR["xres"][par]
            with contextlib.ExitStack() as st:
                xTh = [T(nc, st, f"c_xT{i}", [128, 8, TT + 1], BF16) for i in range(2)]
                fTt = T(nc, st, "c_fT", [128, 8, TT], BF16)
                rTt = T(nc, st, "c_rT", [128, 16, TT], BF16)
                cTt = T(nc, st, "c_cT", [128, 8, TT], BF16)
                W = [T(nc, st, f"c_w{i}", [128, 8, 512], BF16) for i in range(7)]
                btmp = [T(nc, st, f"c_bt{i}", [128, TT], F32) for i in range(2)]
                lnp = load_lnp(st, l, 0)
                ctmp = [T(nc, st, f"c_ct{i}", [128, TT], F32) for i in range(2)]
                ub = [T(nc, st, f"c_ub{i}", [128, TT + 2], F32) for i in range(2)]
                ytmp = [T(nc, st, f"c_yt{i}", [128, TT], F32) for i in range(2)]
                saved = T(nc, st, "c_sv", [128, 8, 2], F32, nres=8)
                u0 = T(nc, st, "c_u0", [128, 8, 2], F32)
                m = T(nc, st, "c_m", [128, 8, TT], F32, nres=8)
                mT = T(nc, st, "c_mT", [128, 8, TT], BF16)
                sgt = [T(nc, st, f"c_sg{i}", [128, TT], F32) for i in range(2)]
                tmp2 = [T(nc, st, f"c_t2{i}", [128, TT], F32) for i in range(2)]
                pres = [T(nc, st, f"c_pre{i}", [128, 4, D], F32, nres=4) for i in range(1)]
                x1T = T(nc, st, "c_x1T", [128, 8, TT], BF16)
                stt = T(nc, st, "c_stt", [128, 2, 6], F32)
                mv = T(nc, st, "c_mv", [128, 2], F32)
                rs = T(nc, st, "c_rs", [128, 2], F32)
                xb4 = T(nc, st, "c_xb4", [128, 4, D], BF16, nres=4)
                ntile = S // TT
                per_tile = []
                for cg in range(2):
                    per_tile += [("in", l, 0, 8, cc_ + cg * 512, 512) for cc_ in (C_CC, C_CX, C_CB)]
                for nk_, wkey_, gcol_ in ((8, "fo", C_GF), (16, "ro", C_GR), (8, "co", C_GC)):
                    for cg in range(2):
                        per_tile.append(("in", l, 0, 8, gcol_ + cg * 512, 512))
                        per_tile += [(wkey_, l, kb * 1024, 8, cg * 512, 512) for kb in range(nk_ // 8)]
                per_tile += [("o", l, 0, 8, cg * 512, 512) for cg in range(2)]
                ws = WStream(W, per_tile * ntile, hold=3)

                def next_w(*spec):
                    return ws.next(spec)

                def loads(t):
                    a0 = t * TT
                    xt = xTh[t % 2]
                    last = (t == ntile - 1)
                    ncol = TT if last else TT + 1
                    if last:
                        P.op("pool", lambda h: h.memset(xt.t[:, :, TT:TT + 1], 0.0), writes=[xt.R])
                    P.dma("sp", xt.t[:, :, 0:ncol], sc["xT"][par][:, a0:a0 + ncol].rearrange("(k p) t -> p k t", p=128),
                          reads=[sR["xT"][par]], writes=() if last else [xt.R], pwrites=[xt.R] if last else ())

                def load_pre_chunk(t, c):
                    r0 = t * TT + c * 128
                    P.dma("sp", pres[0].t[:, c, :], xres_src[r0:r0 + 128, :], reads=[xres_R], writes=[pres[0].Rs[c]])

                acts = {}

                pending = [None]
                for i in range(8):
                    P.op("pool", lambda h, i=i: h.memset(saved.t[:, i, :], 0.0), writes=[saved.Rs[i]])
                loads(0)
                for t in range(ntile):
                    a0 = t * TT
                    xt = xTh[t % 2]
                    if t + 1 < ntile:
                        loads(t + 1)
                    pre = pres[0]
                    for cg in range(2):
                        wc = next_w("in", l, 0, 8, C_CC + cg * 512, 512)
                        wx = next_w("in", l, 0, 8, C_CX + cg * 512, 512)
                        wb_ = next_w("in", l, 0, 8, C_CB + cg * 512, 512)
                        for rb in range(4):
                            blk = cg * 4 + rb
                            ct, u, yt = ctmp[blk % 2], ub[blk % 2], ytmp[blk % 2]
                            wsl = slice(rb * 128, (rb + 1) * 128)
                            if t == 0:
                                bk0, bk0R = take_ps()
                                P.mm([(lambda h, k=k, w_=w_, col=col, bk0=bk0: h.matmul(
                                    bk0[:, col:col + 1], lhsT=w_.t[:, k, wsl], rhs=xt.t[:, k, 0:1],
                                    start=(k == 0), stop=(k == 7))) for col, w_ in ((0, wc), (1, wx)) for k in range(8)],
                                    reads=[wc.R, wx.R, xt.R], writes=[bk0R])
                                P.op("act", lambda h, bk0=bk0: h.activation(out=u0.t[:, blk, 0:1], in_=bk0[:, 0:1], func=AF.Copy),
                                     reads=[bk0R], writes=[u0.R])
                                P.op("dve", lambda h, bk0=bk0, blk=blk: h.tensor_tensor(
                                    out=saved.t[:, blk, 1:2], in0=u0.t[:, blk, 0:1], in1=bk0[:, 1:2], op=ALU.mult),
                                    reads=[bk0R, u0.R], writes=[saved.Rs[blk]])
                            bC, bCR = take_ps()
                            P.mm([(lambda h, k=k, bC=bC: h.matmul(bC[:], lhsT=wc.t[:, k, wsl], rhs=xt.t[:, k, 1:TT + 1],
                                                                  start=(k == 0), stop=(k == 7))) for k in range(8)],
                                 reads=[wc.R, xt.R], writes=[bCR])
                            bX, bXR = take_ps()
                            P.mm([(lambda h, k=k, bX=bX: h.matmul(bX[:], lhsT=wx.t[:, k, wsl], rhs=xt.t[:, k, 1:TT + 1],
                                                                  start=(k == 0), stop=(k == 7))) for k in range(8)],
                                 reads=[wx.R, xt.R], writes=[bXR])
                            bB, bBR = take_ps()
                            P.mm([(lambda h, k=k, bB=bB: h.matmul(bB[:], lhsT=wb_.t[:, k, wsl], rhs=xt.t[:, k, 0:TT],
                                                                  start=(k == 0), stop=(k == 7))) for k in range(8)],
                                 reads=[wb_.R, xt.R], writes=[bBR])
                            P.op("act", lambda h, ct=ct, bC=bC: h.activation(out=ct.t[:], in_=bC[:], func=AF.Copy),
                                 reads=[bCR], writes=[ct.R])
                            bt = btmp[blk % 2]
                            P.op("act", lambda h, bt=bt, bB=bB: h.activation(out=bt.t[:], in_=bB[:], func=AF.Copy),
                                 reads=[bBR], writes=[bt.R])
                            P.op("act", lambda h, u=u, blk=blk: h.activation(out=u.t[:, 0:2], in_=saved.t[:, blk, :], func=AF.Copy),
                                 reads=[saved.Rs[blk]], writes=[u.R])
                            P.op("dve", lambda h, u=u, ct=ct, bX=bX: h.tensor_tensor(out=u.t[:, 2:TT + 2], in0=ct.t[:], in1=bX[:],
                                                                                      op=ALU.mult),
                                 reads=[ct.R, bXR], pwrites=[u.R])
                            P.op("act", lambda h, u=u, blk=blk: h.activation(out=saved.t[:, blk, :], in_=u.t[:, TT:TT + 2], func=AF.Copy),
                                 reads=[u.R], writes=[saved.Rs[blk]])
                            P.op("act", lambda h, u=u, yt=yt, blk=blk: h.activation(
                                out=yt.t[:], in_=u.t[:, 0:TT], func=AF.Copy, scale=cw.t[:, blk, 0:1]),
                                reads=[u.R, cw.R], writes=[yt.R])
                            for j in (1, 2):
                                P.op("dve", lambda h, u=u, yt=yt, blk=blk, j=j: h.scalar_tensor_tensor(
                                    out=yt.t[:], in0=u.t[:, j:TT + j], scalar=cw.t[:, blk, j:j + 1], in1=yt.t[:],
                                    op0=ALU.mult, op1=ALU.add), reads=[u.R, cw.R, yt.R], writes=[yt.R])
                            P.op("dve", lambda h, yt=yt, bt=bt, blk=blk: h.tensor_tensor(out=cTt.t[:, blk, :], in0=yt.t[:], in1=bt.t[:],
                                                                                         op=ALU.mult),
                                 reads=[yt.R, bt.R], pwrites=[cTt.R])
                            for fn in acts.pop(blk, []):
                                fn()
                            if blk == 1:
                                P.dma("sp", fTt.t[:], sc["fT"][:, a0:a0 + TT].rearrange("(k p) t -> p k t", p=128),
                                      reads=[sR["fT"]], writes=[fTt.R])
                            if blk == 2:
                                P.dma("sp", rTt.t[:], sc["rT"][:, a0:a0 + TT].rearrange("(k p) t -> p k t", p=128),
                                      reads=[sR["rT"]], writes=[rTt.R])
                    for bi, (src_t, nk, wkey, gcol) in enumerate(((fTt, 8, "fo", C_GF), (rTt, 16, "ro", C_GR), (cTt, 8, "co", C_GC))):
                        for cg in range(2):
                            wg = next_w("in", l, 0, 8, gcol + cg * 512, 512)
                            wos = []
                            for kb in range(nk // 8):
                                wos.append(next_w(wkey, l, kb * 1024, 8, cg * 512, 512))
                            for rb in range(4):
                                blk = cg * 4 + rb
                                wsl = slice(rb * 128, (rb + 1) * 128)
                                bG, bGR = take_ps()
                                P.mm([(lambda h, k=k, bG=bG: h.matmul(bG[:], lhsT=wg.t[:, k, wsl], rhs=xt.t[:, k, 0:TT],
                                                                      start=(k == 0), stop=(k == 7))) for k in range(8)],
                                     reads=[wg.R, xt.R], writes=[bGR])
                                bO, bOR = take_ps()
                                P.mm([(lambda h, kk=kk, bO=bO: h.matmul(bO[:], lhsT=wos[kk // 8].t[:, kk % 8, wsl],
                                                                        rhs=src_t.t[:, kk, :], start=(kk == 0), stop=(kk == nk - 1)))
                                      for kk in range(nk)], reads=[w_.R for w_ in wos] + [src_t.R], writes=[bOR])
                                if bi == 0 and cg == 0:
                                    if rb == 0:
                                        for fn in acts.pop(8, []):
                                            fn()
                                    load_pre_chunk(t, rb)
                                sg_ = sgt[blk % 2]
                                P.op("act", lambda h, sg_=sg_, bG=bG: h.activation(out=sg_.t[:], in_=bG[:], func=AF.Sigmoid),
                                     reads=[bGR], writes=[sg_.R])
                                if bi == 0:
                                    P.op("dve", lambda h, sg_=sg_, bO=bO, blk=blk: h.tensor_tensor(out=m.t[:, blk, :], in0=sg_.t[:],
                                                                                                   in1=bO[:], op=ALU.mult),
                                         reads=[sg_.R, bOR], writes=[m.Rs[blk]])
                                else:
                                    t2 = tmp2[blk % 2]
                                    P.op("dve", lambda h, sg_=sg_, bO=bO, t2=t2: h.tensor_tensor(out=t2.t[:], in0=sg_.t[:], in1=bO[:],
                                                                                                 op=ALU.mult),
                                         reads=[sg_.R, bOR], writes=[t2.R])
                                    if bi == 1:
                                        P.op("dve", lambda h, t2=t2, blk=blk: h.tensor_tensor(out=m.t[:, blk, :], in0=m.t[:, blk, :],
                                                                                               in1=t2.t[:], op=ALU.add),
                                             reads=[t2.R, m.Rs[blk]], writes=[m.Rs[blk]])
                                    else:
                                        P.op("dve", lambda h, t2=t2, blk=blk: h.tensor_tensor(out=mT.t[:, blk, :], in0=m.t[:, blk, :],
                                                                                               in1=t2.t[:], op=ALU.add),
                                             reads=[t2.R, m.Rs[blk]], writes=[mT.R] if blk == 0 else (),
                                             pwrites=() if blk == 0 else [mT.R])
                    for cg in range(2):
                        w = next_w("o", l, 0, 8, cg * 512, 512)
                        for c in range(4):
                            bank, bR = take_ps()
                            P.mm([(lambda h, k=k, c=c, bank=bank: h.matmul(bank[:], lhsT=mT.t[:, k, c * 128:(c + 1) * 128],
                                                                           rhs=w.t[:, k, :], start=(k == 0), stop=(k == 7)))
                                  for k in range(8)], reads=[w.R, mT.R], writes=[bR])
                            psl = pre.t[:, c, cg * 512:(cg + 1) * 512]
                            P.op("dve", lambda h, psl=psl, bank=bank: h.scalar_tensor_tensor(out=psl, in0=psl, scalar=ALPHA, in1=bank[:],
                                                                                            op0=ALU.mult, op1=ALU.add),
                                 reads=[bR, pre.Rs[c]], writes=[pre.Rs[c]])
                    if debug:
                        P.dma("sp", sc["dmT"][:, a0:a0 + TT].rearrange("(k p) t -> p k t", p=128), mT.t[:], reads=[mT.R], pwrites=[sR["dmT"]])
                        P.dma("sp", sc["dcT"][:, a0:a0 + TT].rearrange("(k p) t -> p k t", p=128), cTt.t[:], reads=[cTt.R], pwrites=[sR["dcT"]])
                        P.dma("sp", sc["dm"][:, a0:a0 + TT].rearrange("(k p) t -> p k t", p=128), m.t[:], reads=m.Rs, pwrites=[sR["dm"]])
                        P.dma("sp", sc["dpre"][a0:a0 + TT, :].rearrange("(c p) d -> p c d", p=128), pre.t[:], reads=pre.Rs, pwrites=[sR["dpre"]])

                    ln_fns, st_fns = ln_math(lnp, pre, sc["x1"][a0:a0 + TT, :], sR["x1"], xb4, stt, mv, rs)
                    acts = {0: [ln_fns[0]], 1: [ln_fns[1]], 2: [ln_fns[2]], 3: [ln_fns[3]],
                            4: [st_fns[0]], 5: [st_fns[1]], 6: [st_fns[2], (lambda: ln_T(xb4, x1T))], 7: [st_fns[3]],
                            8: [(lambda a0=a0: P.dma("sp", sc["x1T"][:, a0:a0 + TT].rearrange("(k p) t -> p k t", p=128), x1T.t[:],
                                                     reads=[x1T.R], pwrites=[sR["x1T"]]))]}
                for k_ in sorted(acts):
                    for fn in acts[k_]:
                        fn()
                P.barrier(label="B2_" + nm)

        def phase_c(nm, S, l, par, is_last):
            sc, sR = scr[nm], scrR[nm]
            dst = yout[nm] if is_last else sc["xres"][1 - par]
            dstR = yR[nm] if is_last else sR["xres"][1 - par]
            with contextlib.ExitStack() as st:
                x1T = [T(nc, st, f"d_x1T{i}", [128, 8, TT], BF16) for i in range(2)]
                pre = [T(nc, st, f"d_pre{i}", [128, 4, D], F32, nres=4) for i in range(2)]
                W = [T(nc, st, f"d_w{i}", [128, 8, 512], BF16) for i in range(8)]
                lnp = load_lnp(st, l, 1)
                hT = T(nc, st, "d_hT", [128, 22, TT], BF16)
                stmp = [T(nc, st, f"d_st{i}", [128, TT], F32) for i in range(2)]
                xTn = T(nc, st, "d_xTn", [128, 8, TT], BF16)
                stt = T(nc, st, "d_stt", [128, 2, 6], F32)
                mv = T(nc, st, "d_mv", [128, 2], F32)
                rs = T(nc, st, "d_rs", [128, 2], F32)
                xb4 = T(nc, st, "d_xb4", [128, 4, D], BF16, nres=4)
                ntile = S // TT
                per_tile = []
                for cg in range(6):
                    wd_ = 512 if cg < 5 else 256
                    per_tile += [("fi", l, 0, 8, cg * 512, wd_), ("fi", l, 0, 8, FF + cg * 512, wd_)]
                for cg in range(2):
                    per_tile += [("fout", l, kb * 1024, nkb, cg * 512, 512) for kb, nkb in enumerate((8, 8, 6))]
                ws = WStream(W, per_tile * ntile, hold=2)

                def next_w(*spec):
                    return ws.next(spec)

                def loads(t):
                    a0 = t * TT
                    xt, pr = x1T[t % 2], pre[t % 2]
                    P.dma("sp", xt.t[:], sc["x1T"][:, a0:a0 + TT].rearrange("(k p) t -> p k t", p=128),
                          reads=[sR["x1T"]], writes=[xt.R])

                def load_pre_chunk(t, c):
                    r0 = t * TT + c * 128
                    pr_ = pre[t % 2]
                    P.dma("sp", pr_.t[:, c, :], sc["x1"][r0:r0 + 128, :], reads=[sR["x1"]], writes=[pr_.Rs[c]])

                loads(0)
                for c_ in range(4):
                    load_pre_chunk(0, c_)
                acts = {}
                for t in range(ntile):
                    a0 = t * TT
                    xt, pr = x1T[t % 2], pre[t % 2]
                    if t + 1 < ntile:
                        loads(t + 1)
                    for cg in range(6):
                        wd = 512 if cg < 5 else 256
                        wgt = next_w("fi", l, 0, 8, cg * 512, wd)
                        wup = next_w("fi", l, 0, 8, FF + cg * 512, wd)
                        for rb in range(wd // 128):
                            j = cg * 4 + rb
                            wsl = slice(rb * 128, (rb + 1) * 128)
                            bG, bGR = take_ps()
                            P.mm([(lambda h, k=k, bG=bG: h.matmul(bG[:], lhsT=wgt.t[:, k, wsl], rhs=xt.t[:, k, :],
                                                                  start=(k == 0), stop=(k == 7))) for k in range(8)],
                                 reads=[wgt.R, xt.R], writes=[bGR])
                            bU, bUR = take_ps()
                            P.mm([(lambda h, k=k, bU=bU: h.matmul(bU[:], lhsT=wup.t[:, k, wsl], rhs=xt.t[:, k, :],
                                                                  start=(k == 0), stop=(k == 7))) for k in range(8)],
                                 reads=[wup.R, xt.R], writes=[bUR])
                            s_ = stmp[j % 2]
                            P.op("act", lambda h, s_=s_, bG=bG: h.activation(out=s_.t[:], in_=bG[:], func=AF.Silu),
                                 reads=[bGR], writes=[s_.R])
                            P.op("dve", lambda h, s_=s_, bU=bU, j=j: h.tensor_tensor(out=hT.t[:, j, :], in0=s_.t[:], in1=bU[:],
                                                                                     op=ALU.mult),
                                 reads=[s_.R, bUR], pwrites=[hT.R])
                            for fn in acts.pop(j, []):
                                fn()
                            if 14 <= j <= 17 and t + 1 < ntile:
                                load_pre_chunk(t + 1, j - 14)
                    for cg in range(2):
                        banks = [take_ps() for _ in range(4)]
                        for kb, nkb in enumerate((8, 8, 6)):
                            w = next_w("fout", l, kb * 1024, nkb, cg * 512, 512)
                            for c in range(4):
                                bank, bR = banks[c]
                                fns = [(lambda h, k=k, c=c, bank=bank, kb=kb, nkb=nkb: h.matmul(
                                    bank[:], lhsT=hT.t[:, kb * 8 + k, c * 128:(c + 1) * 128], rhs=w.t[:, k, :],
                                    start=(kb == 0 and k == 0), stop=(kb == 2 and k == nkb - 1))) for k in range(nkb)]
                                if kb == 0:
                                    P.mm(fns, reads=[w.R, hT.R], writes=[bR])
                                else:
                                    P.mm(fns, reads=[w.R, hT.R], pwrites=[bR])
                        for c in range(4):
                            bank, bR = banks[c]
                            psl = pr.t[:, c, cg * 512:(cg + 1) * 512]
                            P.op("dve", lambda h, psl=psl, bank=bank: h.scalar_tensor_tensor(out=psl, in0=psl, scalar=ALPHA, in1=bank[:],
                                                                                            op0=ALU.mult, op1=ALU.add),
                                 reads=[bR, pr.Rs[c]], writes=[pr.Rs[c]])

                    ln_fns, st_fns = ln_math(lnp, pr, dst[a0:a0 + TT, :], dstR, xb4, stt, mv, rs)
                    acts = {1: [ln_fns[0]], 2: [ln_fns[1]], 3: [ln_fns[2]], 4: [ln_fns[3]],
                            6: [st_fns[0]], 7: [st_fns[1]], 8: [st_fns[2]], 9: [st_fns[3]]}
                    if not is_last:
                        acts[10] = [(lambda: ln_T(xb4, xTn))]
                        acts[13] = [(lambda a0=a0: P.dma("sp", sc["xT"][1 - par][:, a0:a0 + TT].rearrange("(k p) t -> p k t", p=128),
                                                         xTn.t[:], reads=[xTn.R], pwrites=[sR["xT"][1 - par]]))]
                for k_ in sorted(acts):
                    for fn in acts[k_]:
                        fn()
                P.barrier(label="C_" + nm)

        for nm, S in seqs:
            phase_x0(nm, S)
        for l in range(L):
            layer_consts(l)
            par = l % 2
            for nm, S in seqs:
                phase_a(nm, S, l, par)
                phase_f1(nm, S)
                phase_f3(nm, S)
                phase_b1(nm, S, l, par)
                phase_b2(nm, S, l, par)
                phase_c(nm, S, l, par, l == L - 1)
        P.barrier(include_bg=True)
        build.last_ninst = P.ninst
        build.plog = P.plog
    return nc


def run(inputs, seq_map, L, n_cores, debug=False, trace=False):
    seqs = [(nm, a.shape[0]) for nm, a in seq_map[0].items()]
    nc = build(seqs, L, debug=debug)
    smax = max(S for _, S in seqs)
    consts = make_consts([S for _, S in seqs], smax)
    shared = {
        "w_in": inputs["w_in"], "w_fourier_out": inputs["w_fourier_out"], "w_ret_out": inputs["w_ret_out"],
        "w_conv_out": inputs["w_conv_out"], "w_o": inputs["w_o"], "w_ffn_in": inputs["w_ffn_in"],
        "w_ffn_out": inputs["w_ffn_out"], "ret_decay_logit": inputs["ret_decay_logit"].reshape(L, 8),
        "conv_w": inputs["conv_w"], "ln_gain": inputs["ln_gain"], "ln_bias": inputs["ln_bias"],
    }
    shared = {k: np.ascontiguousarray(np.asarray(v, dtype=np.float32)) for k, v in shared.items()}
    shared.update(consts)
    in_maps = []
    for c in range(n_cores):
        m = dict(shared)
        for nm, a in seq_map[c].items():
            m[f"x_{nm}"] = np.ascontiguousarray(np.asarray(a, dtype=np.float32))
        in_maps.append(m)
    res = run_bass_kernel_spmd(nc, in_maps, core_ids=list(range(n_cores)), **({"trace": True} if trace else {}))
    return res


def kernel(x_prompt, x_sample, w_in, ret_decay_logit, conv_w, w_fourier_out, w_ret_out,
           w_conv_out, w_o, ln_gain, ln_bias, w_ffn_in, w_ffn_out):
    inputs = dict(w_in=w_in, ret_decay_logit=ret_decay_logit, conv_w=conv_w, w_fourier_out=w_fourier_out,
                  w_ret_out=w_ret_out, w_conv_out=w_conv_out, w_o=w_o, ln_gain=ln_gain, ln_bias=ln_bias,
                  w_ffn_in=w_ffn_in, w_ffn_out=w_ffn_out)
    x_prompt = np.asarray(x_prompt)
    x_sample = np.asarray(x_sample)
    n = 8
    nb_p = x_prompt.shape[0]
    seq_map = [{"s": x_sample[c], "p": x_prompt[c % nb_p]} for c in range(n)]
    res = run(inputs, seq_map, int(np.asarray(w_in).shape[0]), n)
    y_sample = np.stack([res.results[c]["y_s"] for c in range(n)], axis=0).astype(np.float32)
    y_prompt = np.stack([res.results[c]["y_p"] for c in range(nb_p)], axis=0).astype(np.float32)
    return (y_prompt, y_sample)
```

```python
import contextlib
import math
import numpy as np
import ml_dtypes
import concourse.bass as bass
import concourse.mybir as mybir
from concourse.bass_utils import run_bass_kernel_spmd

F32 = mybir.dt.float32
BF16 = mybir.dt.bfloat16
AF = mybir.ActivationFunctionType
ALU = mybir.AluOpType

D = 1024
NH = 4
DK = 256
DV = 512
RV = NH * DV
FF = 2816
INC = 13312
C_F, C_Q, C_K, C_V, C_G = 0, 1024, 2048, 3072, 5120
C_CB, C_CC, C_CX, C_GF, C_GR, C_GC = 7168, 8192, 9216, 10240, 11264, 12288
DEPTH_REF = 4
ALPHA = (2.0 * DEPTH_REF) ** 0.25
LN_EPS = 1e-5
HN_EPS = 1e-6
TT = 512


class Res:
    __slots__ = ("w", "r", "base", "name")

    def __init__(self, name=""):
        self.w = {}
        self.r = {}
        self.base = {}
        self.name = name


class Eng:
    def __init__(self, name, h, sem):
        self.name = name
        self.h = h
        self.sem = sem
        self.cnt = 0
        self.seen = {}


class Prog:
    NDMA = 20

    def __init__(self, nc, stack):
        self.nc = nc
        self.sems = []

        def mk(nm):
            s = stack.enter_context(nc.semaphore(nm))
            self.sems.append(s)
            return len(self.sems) - 1

        self.E = {}
        for nm, h in (("pe", nc.tensor), ("act", nc.scalar), ("dve", nc.vector),
                      ("pool", nc.gpsimd), ("sp", nc.sync)):
            self.E[nm] = Eng(nm, h, mk("s_" + nm))
        self.dq = {}
        for q, e in (("sp", "sp"), ("bg", "pool")):
            self.dq[q] = dict(sems=[mk(f"d_{q}{i}") for i in range(self.NDMA)], n=0, eng=e)
        self.ninst = 0
        self.npe = 0
        self.plog = []

    def _need(self, reads, writes, pwrites):
        need = {}
        for res in reads:
            for s, v in res.w.items():
                if need.get(s, 0) < v:
                    need[s] = v
        for res in writes:
            for d in (res.w, res.r):
                for s, v in d.items():
                    if need.get(s, 0) < v:
                        need[s] = v
        for res in pwrites:
            for d in (res.r, res.base):
                for s, v in d.items():
                    if need.get(s, 0) < v:
                        need[s] = v
        return need

    def _waits(self, eng, need):
        seen = eng.seen
        for s, v in need.items():
            if seen.get(s, 0) < v:
                eng.h.wait_ge(self.sems[s], v)
                seen[s] = v
                self.ninst += 1

    def _mark(self, tk, reads, writes, pwrites):
        s, v = tk
        for res in reads:
            res.r[s] = v
        for res in writes:
            base = dict(res.w)
            for s2, v2 in res.r.items():
                if base.get(s2, 0) < v2:
                    base[s2] = v2
            res.base = base
            res.w = {s: v}
            res.r = {}
        for res in pwrites:
            res.w[s] = v

    def op(self, en, fn, reads=(), writes=(), pwrites=()):
        eng = self.E[en]
        self._waits(eng, self._need(reads, writes, pwrites))
        ins = fn(eng.h)
        eng.cnt += 1
        ins.then_inc(self.sems[eng.sem], 1)
        self.ninst += 1
        self._mark((eng.sem, eng.cnt), reads, writes, pwrites)

    def mm(self, fns, reads=(), writes=(), pwrites=()):
        eng = self.E["pe"]
        self._waits(eng, self._need(reads, writes, pwrites))
        ins = None
        for fn in fns:
            ins = fn(eng.h)
            self.ninst += 1
            self.npe += 1
        eng.cnt += 1
        ins.then_inc(self.sems[eng.sem], 1)
        self._mark((eng.sem, eng.cnt), reads, writes, pwrites)

    def dma(self, q, out, in_, reads=(), writes=(), pwrites=(), **kw):
        pool = self.dq[q]
        eng = self.E[pool["eng"]]
        i = pool["n"]
        pool["n"] += 1
        s = pool["sems"][i % self.NDMA]
        v = 16 * (i // self.NDMA + 1)
        need = self._need(reads, writes, pwrites)
        if v > 16 and need.get(s, 0) < v - 16:
            need[s] = v - 16
        self._waits(eng, need)
        eng.h.dma_start(out=out, in_=in_, **kw).then_inc(self.sems[s], 16)
        self.ninst += 1
        self._mark((s, v), reads, writes, pwrites)

    def barrier(self, include_bg=False, label=None):
        if label:
            self.plog.append((label, self.npe))
        latest = {}
        for e in self.E.values():
            if e.cnt:
                latest[e.sem] = e.cnt
        for q, pool in self.dq.items():
            if q == "bg" and not include_bg:
                continue
            n = pool["n"]
            for k, s in enumerate(pool["sems"]):
                c = (n - k + self.NDMA - 1) // self.NDMA if n > k else 0
                if c:
                    latest[s] = 16 * c
        for e in self.E.values():
            self._waits(e, latest)


class T:
    uid = [0]

    def __init__(self, nc, st, name, shape, dt, nres=1):
        T.uid[0] += 1
        name = f"sb{T.uid[0]}_{name}"
        self.t = st.enter_context(nc.sbuf_tensor(name, list(shape), dt))
        self.R = Res(name)
        self.Rs = [Res(f"{name}{i}") for i in range(nres)] if nres > 1 else [self.R]


def _bf(a):
    return np.ascontiguousarray(a.astype(np.float32)).astype(ml_dtypes.bfloat16)


def make_consts(seq_lens, smax):
    c = {}
    c["ident"] = _bf(np.eye(128))
    j = np.arange(128)[:, None]
    i = np.arange(128)[None, :]
    c["mask"] = np.stack([(i >= j), (j > i)]).astype(np.float32)
    p = np.arange(128, dtype=np.float64)
    c["pcols"] = np.stack([p + 1, -(p + 1), 128 - p, p - 128], axis=1).astype(np.float32)
    a = 2 * np.pi * np.outer(np.arange(256), np.arange(256)) / 256.0
    c["chan"] = _bf(np.concatenate([np.cos(a), -np.sin(a)], axis=1) / 16.0)
    for S in sorted(set(seq_lens)):
        n2c = S // 128
        n1 = np.arange(128)[:, None]
        k1 = np.arange(128)[None, :]
        mats = np.zeros((n2c, 128, 3, 128), np.float64)
        for n2 in range(n2c):
            ang = 2 * np.pi * (n1 * k1 / 128.0 + n2 * k1 / float(S))
            mats[n2, :, 0] = np.cos(ang)
            mats[n2, :, 1] = np.sin(ang)
            mats[n2, :, 2] = -np.sin(ang)
        c[f"t1_{S}"] = _bf(mats / math.sqrt(128.0))
        a2 = 2 * np.pi * np.outer(np.arange(n2c), np.arange(n2c)) / float(n2c)
        c[f"t3_{S}"] = _bf(np.concatenate([np.cos(a2), np.sin(a2)], axis=0) / math.sqrt(n2c))
    inv = 1.0 / (10000.0 ** np.linspace(0.0, 1.0, 128, dtype=np.float32))
    ang = inv[:, None].astype(np.float32) * np.arange(smax, dtype=np.float32)[None, :]
    c["cos"] = np.cos(ang).astype(np.float32)
    c["sin"] = np.sin(ang).astype(np.float32)
    return c


def build(seqs, L, debug=False):
    nc = bass.Bass("TRN2", target_bir_lowering=False)
    smax = max(S for _, S in seqs)

    def din(name, shape, dt=F32):
        return nc.dram_tensor(name, list(shape), dt, kind="ExternalInput").ap()

    def dscr(name, shape, dt):
        return nc.dram_tensor(name, list(shape), dt,
                              kind="ExternalOutput" if debug else "Internal").ap()

    xin = {nm: din(f"x_{nm}", [S, D]) for nm, S in seqs}
    yout = {nm: nc.dram_tensor(f"y_{nm}", [S, D], F32, kind="ExternalOutput").ap() for nm, S in seqs}
    wsrc = {
        "in": din("w_in", [L, D, INC]), "fo": din("w_fourier_out", [L, D, D]),
        "ro": din("w_ret_out", [L, RV, D]), "co": din("w_conv_out", [L, D, D]),
        "o": din("w_o", [L, D, D]), "fi": din("w_ffn_in", [L, D, 2 * FF]),
        "fout": din("w_ffn_out", [L, FF, D]),
    }
    decay_in = din("ret_decay_logit", [L, 8])
    convw_in = din("conv_w", [L, 3, D])
    lng_in = din("ln_gain", [L, 2, D])
    lnb_in = din("ln_bias", [L, 2, D])
    c_ident = din("ident", [128, 128], BF16)
    c_mask = din("mask", [2, 128, 128])
    c_pcols = din("pcols", [128, 4])
    c_chan = din("chan", [256, 512], BF16)
    c_cos = din("cos", [128, smax])
    c_sin = din("sin", [128, smax])
    c_t1, c_t3 = {}, {}
    for S in sorted(set(S for _, S in seqs)):
        c_t1[S] = din(f"t1_{S}", [S // 128, 128, 3, 128], BF16)
        c_t3[S] = din(f"t3_{S}", [2 * (S // 128), S // 128], BF16)

    wb = {k: dscr("wb_" + k, v.shape, BF16) for k, v in wsrc.items()}
    wbR = {k: [Res(f"wb_{k}{l}") for l in range(L)] for k in wsrc}
    scr = {}
    for nm, S in seqs:
        n2c = S // 128
        scr[nm] = dict(
            xT=[dscr(f"xT{i}_{nm}", [D, S], BF16) for i in range(2)],
            xres=[dscr(f"xres{i}_{nm}", [S, D], F32) for i in range(2)],
            Zs=dscr(f"Zs_{nm}", [S, 2048], BF16),
            Ys=dscr(f"Ys_{nm}", [128, 2, n2c, D], BF16),
            fT=dscr(f"fT_{nm}", [D, S], BF16),
            qT=dscr(f"qT_{nm}", [D, S], BF16), kT=dscr(f"kT_{nm}", [D, S], BF16),
            ktm=dscr(f"ktm_{nm}", [S, D], BF16), vtm=dscr(f"vtm_{nm}", [S, RV], BF16),
            of=dscr(f"of_{nm}", [S, RV], F32), rT=dscr(f"rT_{nm}", [RV, S], BF16),
            x1=dscr(f"x1_{nm}", [S, D], F32), x1T=dscr(f"x1T_{nm}", [D, S], BF16),
        )
        if debug:
            scr[nm].update(dmT=dscr(f"dmT_{nm}", [D, S], BF16), dcT=dscr(f"dcT_{nm}", [D, S], BF16),
                           dpre=dscr(f"dpre_{nm}", [S, D], F32), dm=dscr(f"dm_{nm}", [D, S], F32))
    scrR = {nm: {k: Res(f"{k}_{nm}") for k in scr[nm]} for nm, _ in seqs}
    for nm, _ in seqs:
        scrR[nm]["xT"] = [Res("xT0"), Res("xT1")]
        scrR[nm]["xres"] = [Res("xr0"), Res("xr1")]
    yR = {nm: Res("y" + nm) for nm, _ in seqs}
    inR = Res("inputs")

    with contextlib.ExitStack() as gst:
        P = Prog(nc, gst)
        psb = [gst.enter_context(nc.psum_tensor(f"psb{i}", [128, 512], F32)) for i in range(8)]
        psR = [Res(f"ps{i}") for i in range(8)]
        pstate = dict(i=0, ev=0)

        def take_ps():
            i = pstate["i"]
            pstate["i"] = (i + 1) % 8
            return psb[i], psR[i]

        def evq():
            pstate["ev"] ^= 1
            return "act" if pstate["ev"] else "dve"

        def copy_op(en, out, in_):
            if en == "act":
                return lambda h: h.activation(out=out, in_=in_, func=AF.Copy)
            return lambda h: h.tensor_copy(out=out, in_=in_)

        ident = T(nc, gst, "ident", [128, 128], BF16)
        maskT = T(nc, gst, "maskT", [128, 2, 128], F32)
        pcols = T(nc, gst, "pcols", [128, 4], F32)
        chan = T(nc, gst, "chan", [128, 2, 512], BF16)
        dec = T(nc, gst, "dec", [128, 24], F32)
        dtmp = T(nc, gst, "dtmp", [128, 24], F32)
        cw = T(nc, gst, "cw", [128, 8, 3], F32)
        epsc = T(nc, gst, "epsc", [128, 2], F32)
        P.dma("sp", ident.t[:], c_ident[:], reads=[inR], writes=[ident.R])
        P.dma("sp", maskT.t[:], c_mask.rearrange("d j i -> j d i"), reads=[inR], writes=[maskT.R])
        P.dma("sp", pcols.t[:], c_pcols[:], reads=[inR], writes=[pcols.R])
        P.dma("sp", chan.t[:], c_chan.rearrange("(k p) c -> p k c", p=128), reads=[inR], writes=[chan.R])
        P.op("dve", lambda h: h.memset(epsc.t[:, 0:1], LN_EPS), pwrites=[epsc.R])
        P.op("dve", lambda h: h.memset(epsc.t[:, 1:2], HN_EPS), pwrites=[epsc.R])

        for l in range(L):
            for k in ("in", "fi", "fout", "ro", "fo", "co", "o"):
                rows = wsrc[k].shape[1]
                for r0 in range(0, rows, 256):
                    r1 = min(rows, r0 + 256)
                    P.dma("bg", wb[k][l, r0:r1, :], wsrc[k][l, r0:r1, :], reads=[inR], pwrites=[wbR[k][l]])

        def load_w(wt, key, l, r0, nk, c0, ncols):
            src = wb[key][l, r0:r0 + nk * 128, c0:c0 + ncols].rearrange("(k p) c -> p k c", p=128)
            P.dma("sp", wt.t[:, 0:nk, 0:ncols], src, reads=[wbR[key][l]], writes=[wt.R])

        class WStream:
            def __init__(self, tiles, reqs, hold=1):
                self.tiles, self.reqs, self.issued, self.i, self.hold = tiles, reqs, 0, 0, hold

            def next(self, spec):
                n = len(self.tiles)
                assert self.reqs[self.i] == spec, (self.i, self.reqs[self.i], spec)
                while self.issued < len(self.reqs) and self.issued <= self.i + n - self.hold:
                    key, l_, r0, nk, c0, ncols = self.reqs[self.issued]
                    load_w(self.tiles[self.issued % n], key, l_, r0, nk, c0, ncols)
                    self.issued += 1
                w = self.tiles[self.i % n]
                self.i += 1
                return w

        def transposes(src_fn, srcR, nblk, dst_fn, dstR):
            for g in range(nblk // 4):
                bank, bR = take_ps()
                P.mm([(lambda h, j=j: h.matmul(bank[:, j * 128:(j + 1) * 128], lhsT=src_fn(g * 4 + j),
                                                rhs=ident.t[:], start=True, stop=True)) for j in range(4)],
                     reads=[srcR, ident.R], writes=[bR])
                en = evq()
                P.op(en, copy_op(en, dst_fn(g), bank[:].rearrange("p (j c) -> p j c", j=4)),
                     reads=[bR], pwrites=[dstR])

        def layer_consts(l):
            P.dma("sp", dtmp.t[:, 0:8], decay_in[l:l + 1, :].partition_broadcast(128), reads=[inR], writes=[dtmp.R])
            P.op("act", lambda h: h.activation(out=dtmp.t[:, 8:16], in_=dtmp.t[:, 0:8], func=AF.Exp, scale=-1.0),
                 reads=[dtmp.R], writes=[dtmp.R])
            P.op("dve", lambda h: h.tensor_scalar(out=dtmp.t[:, 8:16], in0=dtmp.t[:, 8:16], scalar1=1.0, scalar2=None,
                                                  op0=ALU.add), reads=[dtmp.R], writes=[dtmp.R])
            P.op("act", lambda h: h.activation(out=dtmp.t[:, 16:24], in_=dtmp.t[:, 8:16], func=AF.Ln),
                 reads=[dtmp.R], writes=[dtmp.R])
            lnv = dtmp.t[:, 16:24]
            for d in range(2):
                sl = slice(16 + 4 * d, 20 + 4 * d)
                rs_col = 1 if d == 0 else 3
                cs_col = 0 if d == 0 else 2
                P.op("act", lambda h, d=d, sl=sl, c=rs_col: h.activation(
                    out=dec.t[:, 4 * d:4 * d + 4], in_=dtmp.t[:, sl], func=AF.Exp, scale=pcols.t[:, c:c + 1]),
                    reads=[dtmp.R, pcols.R], writes=[dec.R])
                P.op("act", lambda h, d=d, sl=sl, c=cs_col: h.activation(
                    out=dec.t[:, 8 + 4 * d:12 + 4 * d], in_=dtmp.t[:, sl], func=AF.Exp, scale=pcols.t[:, c:c + 1]),
                    reads=[dtmp.R, pcols.R], writes=[dec.R])
                P.op("act", lambda h, d=d, sl=sl: h.activation(
                    out=dec.t[:, 16 + 4 * d:20 + 4 * d], in_=dtmp.t[:, sl], func=AF.Exp, scale=-128.0),
                    reads=[dtmp.R], writes=[dec.R])
            P.op("dve", lambda h: h.tensor_scalar(out=dec.t[:, 8:16], in0=dec.t[:, 8:16], scalar1=1.0 / 16.0,
                                                  scalar2=None, op0=ALU.mult), reads=[dec.R], writes=[dec.R])
            for j in range(3):
                P.dma("sp", cw.t[:, :, j:j + 1], convw_in[l, j:j + 1, :].rearrange("o (b p) -> p b o", p=128),
                      reads=[inR], pwrites=[cw.R], allow_slow_non_contiguous=True)

        def load_lnp(st, l, which):
            lnp = T(nc, st, "lnp", [128, 2, D], F32)
            P.dma("sp", lnp.t[:, 0, :], lng_in[l, which:which + 1, :].partition_broadcast(128), reads=[inR], pwrites=[lnp.R])
            P.dma("sp", lnp.t[:, 1, :], lnb_in[l, which:which + 1, :].partition_broadcast(128), reads=[inR], pwrites=[lnp.R])
            return lnp

        def ln_math(lnp, pre, dst_rows, dstR_f, xb4, stt, mv, rs):
            g_ap = lnp.t[:, 0, :]
            b_ap = lnp.t[:, 1, :]
            for c in range(4):
                pr = pre.Rs[c]
                xc = pre.t[:, c, :]
                for hh in range(2):
                    P.op("dve", lambda h, hh=hh, xc=xc: h.bn_stats(out=stt.t[:, hh, :], in_=xc[:, hh * 512:(hh + 1) * 512]),
                         reads=[pr], pwrites=[stt.R])
                P.op("dve", lambda h: h.bn_aggr(out=mv.t[:], in_=stt.t[:, 0:2, :]), reads=[stt.R], writes=[mv.R])
                P.op("act", lambda h: h.activation(out=rs.t[:, 0:1], in_=mv.t[:, 1:2], func=AF.Sqrt, bias=epsc.t[:, 0:1]),
                     reads=[mv.R, epsc.R], writes=[rs.R])
                P.op("dve", lambda h: h.reciprocal(out=rs.t[:, 1:2], in_=rs.t[:, 0:1]), reads=[rs.R], writes=[rs.R])
                P.op("dve", lambda h, xc=xc: h.scalar_tensor_tensor(out=xc, in0=xc, scalar=mv.t[:, 0:1], in1=g_ap,
                                                                    op0=ALU.subtract, op1=ALU.mult),
                     reads=[pr, mv.R, lnp.R], writes=[pr])
                P.op("dve", lambda h, xc=xc: h.scalar_tensor_tensor(out=xc, in0=xc, scalar=rs.t[:, 1:2], in1=b_ap,
                                                                    op0=ALU.mult, op1=ALU.add),
                     reads=[pr, rs.R, lnp.R], writes=[pr])
                P.op("act", lambda h, xc=xc, c=c: h.activation(out=xb4.t[:, c, :], in_=xc, func=AF.Copy),
                     reads=[pr], writes=[xb4.Rs[c]])

            def stores():
                for c in range(4):
                    P.dma("sp", dst_rows[c * 128:(c + 1) * 128, :], pre.t[:, c, :], reads=[pre.Rs[c]], pwrites=[dstR_f])
            return stores

        def ln_T(xb4, xTt):
            for c in range(4):
                transposes(lambda blk, c=c: xb4.t[:, c, blk * 128:(blk + 1) * 128], xb4.Rs[c], 8,
                           lambda g, c=c: xTt.t[:, g * 4:(g + 1) * 4, c * 128:(c + 1) * 128], xTt.R)

        def phase_x0(nm, S):
            with contextlib.ExitStack() as st:
                xc = [T(nc, st, f"x0c{i}", [128, D], F32) for i in range(2)]
                xb = [T(nc, st, f"x0b{i}", [128, D], BF16) for i in range(2)]
                xTt = [T(nc, st, f"x0T{i}", [128, 8, TT], BF16) for i in range(2)]
                for t in range(S // TT):
                    xt = xTt[t % 2]
                    for c in range(4):
                        i = (t * 4 + c) % 2
                        r0 = t * TT + c * 128
                        P.dma("sp", xc[i].t[:], xin[nm][r0:r0 + 128, :], reads=[inR], writes=[xc[i].R])
                        P.op("act", lambda h, i=i: h.activation(out=xb[i].t[:], in_=xc[i].t[:], func=AF.Copy),
                             reads=[xc[i].R], writes=[xb[i].R])
                        transposes(lambda blk, i=i: xb[i].t[:, blk * 128:(blk + 1) * 128], xb[i].R, 8,
                                   lambda g, c=c: xt.t[:, g * 4:(g + 1) * 4, c * 128:(c + 1) * 128], xt.R)
                    P.dma("sp", scr[nm]["xT"][0][:, t * TT:(t + 1) * TT].rearrange("(k p) t -> p k t", p=128),
                          xt.t[:], reads=[xt.R], pwrites=[scrR[nm]["xT"][0]])
                P.barrier(label="X0_" + nm)

        def ret_prep(d, ci, qTt, kTt, v_fn, vR, vts, PTs):
            cs = slice(ci * 128, (ci + 1) * 128)
            for h in range(4):
                idx = d * 4 + h
                P.op("act", lambda hh, h=h, idx=idx: hh.activation(out=vts[h].t[:], in_=v_fn(h), func=AF.Copy,
                                                                    scale=dec.t[:, 8 + idx:9 + idx]),
                     reads=[vR, dec.R], writes=[vts[h].R])
            bankS, bSR = take_ps()
            P.mm([(lambda hh, h=h, kk=kk: hh.matmul(bankS[:, h * 128:(h + 1) * 128], lhsT=kTt.t[:, 2 * h + kk, cs],
                                                    rhs=qTt.t[:, 2 * h + kk, cs], start=(kk == 0), stop=(kk == 1)))
                  for h in range(4) for kk in range(2)], reads=[kTt.R, qTt.R], writes=[bSR])
            for h in range(4):
                P.op("dve", lambda hh, h=h: hh.tensor_tensor(out=PTs[h].t[:], in0=bankS[:, h * 128:(h + 1) * 128],
                                                             in1=maskT.t[:, d, :], op=ALU.mult),
                     reads=[bSR, maskT.R], writes=[PTs[h].R])

        def ret_main(d, ci, qTt, kTt, ktm_ap, ktmR, vts, PTs, U, Sbf, o_fn, oR, accumulate, mid=None):
            cs = slice(ci * 128, (ci + 1) * 128)
            for hp in range(2):
                heads = (2 * hp, 2 * hp + 1)
                kvb = {}
                for h in heads:
                    for kk in range(2):
                        bank3, b3R = take_ps()
                        P.mm([lambda hh, kk=kk, h=h, bank3=bank3: hh.matmul(
                            bank3[:], lhsT=ktm_ap[:, h * 256 + kk * 128:h * 256 + (kk + 1) * 128], rhs=vts[h].t[:],
                            start=True, stop=True)], reads=[ktmR, vts[h].R], writes=[b3R])
                        kvb[(h, kk)] = (bank3, b3R)
                ob = {}
                for h in heads:
                    bank2, b2R = take_ps()
                    P.mm([lambda hh, h=h, bank2=bank2: hh.matmul(bank2[:], lhsT=PTs[h].t[:], rhs=vts[h].t[:], start=True, stop=False),
                          lambda hh, h=h, bank2=bank2: hh.matmul(bank2[:], lhsT=qTt.t[:, 2 * h, cs], rhs=Sbf.t[:, h, 0, :],
                                                                 start=False, stop=False),
                          lambda hh, h=h, bank2=bank2: hh.matmul(bank2[:], lhsT=qTt.t[:, 2 * h + 1, cs], rhs=Sbf.t[:, h, 1, :],
                                                                 start=False, stop=True)],
                         reads=[PTs[h].R, vts[h].R, qTt.R, Sbf.Rs[2 * h], Sbf.Rs[2 * h + 1]], writes=[b2R])
                    ob[h] = (bank2, b2R)
                if hp == 1 and mid is not None:
                    mid()
                for h in heads:
                    idx = d * 4 + h
                    bank2, b2R = ob[h]
                    o_ap = o_fn(h)
                    if accumulate:
                        P.op("dve", lambda hh, o_ap=o_ap, bank2=bank2, idx=idx: hh.scalar_tensor_tensor(
                            out=o_ap, in0=bank2[:], scalar=dec.t[:, idx:idx + 1], in1=o_ap, op0=ALU.mult, op1=ALU.add),
                            reads=[b2R, dec.R, oR], writes=[oR])
                    else:
                        P.op("act", lambda hh, o_ap=o_ap, bank2=bank2, idx=idx: hh.activation(
                            out=o_ap, in_=bank2[:], func=AF.Copy, scale=dec.t[:, idx:idx + 1]),
                            reads=[b2R, dec.R], pwrites=[oR])
                    for kk in range(2):
                        bank3, b3R = kvb[(h, kk)]
                        uR = U.Rs[2 * h + kk]
                        P.op("dve", lambda hh, kk=kk, h=h, bank3=bank3, idx=idx: hh.scalar_tensor_tensor(
                            out=U.t[:, h, kk, :], in0=U.t[:, h, kk, :], scalar=dec.t[:, 16 + idx:17 + idx], in1=bank3[:],
                            op0=ALU.mult, op1=ALU.add), reads=[b3R, dec.R, uR], writes=[uR])
                        P.op("act", lambda hh, kk=kk, h=h, idx=idx: hh.activation(
                            out=Sbf.t[:, h, kk, :], in_=U.t[:, h, kk, :], func=AF.Copy, scale=dec.t[:, 16 + idx:17 + idx]),
                            reads=[uR, dec.R], writes=[Sbf.Rs[2 * h + kk]])

        def init_state(U, Sbf):
            for i in range(8):
                P.op("dve", lambda hh, i=i: hh.memset(U.t[:, i // 2, i % 2, :], 0.0), writes=[U.Rs[i]])
                P.op("pool", lambda hh, i=i: hh.memset(Sbf.t[:, i // 2, i % 2, :], 0.0), writes=[Sbf.Rs[i]])

        def phase_a(nm, S, l, par):
            sc, sR = scr[nm], scrR[nm]
            with contextlib.ExitStack() as st:
                xTt = [T(nc, st, f"a_xT{i}", [128, 8, TT], BF16) for i in range(2)]
                W = [T(nc, st, f"a_w{i}", [128, 8, 512], BF16) for i in range(4)]
                cs_t = [T(nc, st, f"a_cs{i}", [128, 2, TT], F32) for i in range(2)]
                uT = T(nc, st, "a_uT", [128, 8, TT], BF16)
                qTt = T(nc, st, "a_qT", [128, 8, TT], BF16)
                kTt = T(nc, st, "a_kT", [128, 8, TT], BF16)
                Zt = [T(nc, st, f"a_Z{i}", [128, 2048], BF16) for i in range(2)]
                ktm = [T(nc, st, f"a_ktm{i}", [128, D], BF16) for i in range(4)]
                vtm = T(nc, st, "a_vtm", [128, 4, RV], BF16, nres=4)
                vt = [T(nc, st, f"a_vt{i}", [128, DV], BF16) for i in range(8)]
                PT = [T(nc, st, f"a_PT{i}", [128, 128], BF16) for i in range(8)]
                U = T(nc, st, "a_U", [128, 4, 2, DV], F32, nres=8)
                Sbf = T(nc, st, "a_Sbf", [128, 4, 2, DV], BF16, nres=8)
                ot = [T(nc, st, f"a_o{i}", [128, RV], F32) for i in range(2)]
                rt = [T(nc, st, f"a_rt{i}", [128, TT], F32) for i in range(8)]
                ntile = S // TT
                per_tile = ([("in", l, 0, 8, C_F + cg * 512, 512) for cg in range(2)]
                            + [("in", l, 0, 8, C_Q + cg * 512, 512) for cg in range(2)]
                            + [("in", l, 0, 8, C_K + cg * 512, 512) for cg in range(2)]
                            + [("in", l, 0, 8, C_V + cg * 512, 512) for cg in range(4)])
                ws = WStream(W, per_tile * ntile)

                def loads(t):
                    xt = xTt[t % 2]
                    cst = cs_t[t % 2]
                    a0 = t * TT
                    P.dma("sp", xt.t[:], sc["xT"][par][:, a0:a0 + TT].rearrange("(k p) t -> p k t", p=128),
                          reads=[sR["xT"][par]], writes=[xt.R])
                    P.dma("sp", cst.t[:, 0, :], c_cos[:, a0:a0 + TT], reads=[inR], writes=[cst.R])
                    P.dma("sp", cst.t[:, 1, :], c_sin[:, a0:a0 + TT], reads=[inR], pwrites=[cst.R])

                init_state(U, Sbf)
                loads(0)
                for t in range(ntile):
                    xt = xTt[t % 2]
                    cst = cs_t[t % 2]
                    a0 = t * TT
                    if t + 1 < ntile:
                        loads(t + 1)
                    for cg in range(2):
                        w = ws.next(("in", l, 0, 8, C_F + cg * 512, 512))
                        for rb in range(4):
                            bank, bR = take_ps()
                            P.mm([(lambda h, k=k, rb=rb, w=w: h.matmul(bank[:], lhsT=w.t[:, k, rb * 128:(rb + 1) * 128],
                                                                       rhs=xt.t[:, k, :], start=(k == 0), stop=(k == 7)))
                                  for k in range(8)], reads=[w.R, xt.R], writes=[bR])
                            en = evq()
                            P.op(en, copy_op(en, uT.t[:, cg * 4 + rb, :], bank[:]), reads=[bR], pwrites=[uT.R])
                    for qk, (c0, dstT) in enumerate(((C_Q, qTt), (C_K, kTt))):
                        for cg in range(2):
                            w = ws.next(("in", l, 0, 8, c0 + cg * 512, 512))
                            for hh2 in range(2):
                                head = cg * 2 + hh2
                                banks = []
                                for half in range(2):
                                    rb = hh2 * 2 + half
                                    bank, bR = take_ps()
                                    P.mm([(lambda h, k=k, rb=rb, w=w, bank=bank: h.matmul(
                                        bank[:], lhsT=w.t[:, k, rb * 128:(rb + 1) * 128], rhs=xt.t[:, k, :],
                                        start=(k == 0), stop=(k == 7))) for k in range(8)],
                                        reads=[w.R, xt.R], writes=[bR])
                                    banks.append((bank, bR))
                                (b1, b1R), (b2, b2R) = banks
                                ra, rb_, rc, rd = (rt[(qk * 4 + j) % 8] for j in range(4))
                                cosap, sinap = cst.t[:, 0, :], cst.t[:, 1, :]
                                P.op("dve", lambda h, b1=b1, ra=ra: h.tensor_tensor(out=ra.t[:], in0=b1[:], in1=cosap, op=ALU.mult),
                                     reads=[b1R, cst.R], writes=[ra.R])
                                P.op("dve", lambda h, b2=b2, rb_=rb_: h.tensor_tensor(out=rb_.t[:], in0=b2[:], in1=sinap, op=ALU.mult),
                                     reads=[b2R, cst.R], writes=[rb_.R])
                                P.op("dve", lambda h, b1=b1, rc=rc: h.tensor_tensor(out=rc.t[:], in0=b1[:], in1=sinap, op=ALU.mult),
                                     reads=[b1R, cst.R], writes=[rc.R])
                                P.op("dve", lambda h, b2=b2, rd=rd: h.tensor_tensor(out=rd.t[:], in0=b2[:], in1=cosap, op=ALU.mult),
                                     reads=[b2R, cst.R], writes=[rd.R])
                                P.op("pool", lambda h, ra=ra, rb_=rb_, head=head, dstT=dstT: h.tensor_tensor(
                                    out=dstT.t[:, 2 * head, :], in0=ra.t[:], in1=rb_.t[:], op=ALU.subtract),
                                    reads=[ra.R, rb_.R], pwrites=[dstT.R])
                                P.op("pool", lambda h, rc=rc, rd=rd, head=head, dstT=dstT: h.tensor_tensor(
                                    out=dstT.t[:, 2 * head + 1, :], in0=rc.t[:], in1=rd.t[:], op=ALU.add),
                                    reads=[rc.R, rd.R], pwrites=[dstT.R])
                    late = []
                    late.append(lambda a0=a0: P.dma("sp", sc["qT"][:, a0:a0 + TT].rearrange("(k p) t -> p k t", p=128), qTt.t[:],
                                                    reads=[qTt.R], pwrites=[sR["qT"]]))
                    late.append(lambda a0=a0: P.dma("sp", sc["kT"][:, a0:a0 + TT].rearrange("(k p) t -> p k t", p=128), kTt.t[:],
                                                    reads=[kTt.R], pwrites=[sR["kT"]]))
                    for c in range(4):
                        zt = Zt[c % 2]
                        for g in range(4):
                            bank, bR = take_ps()
                            P.mm([(lambda h, kk=kk, g=g, c=c: h.matmul(bank[:], lhsT=uT.t[:, 2 * g + kk, c * 128:(c + 1) * 128],
                                                                       rhs=chan.t[:, kk, :], start=(kk == 0), stop=(kk == 1)))
                                  for kk in range(2)], reads=[uT.R, chan.R], writes=[bR])
                            en = evq()
                            dst = zt.t[:].rearrange("p (r c) -> p r c", r=2)[:, :, g * 256:(g + 1) * 256]
                            P.op(en, copy_op(en, dst, bank[:].rearrange("p (r c) -> p r c", r=2)),
                                 reads=[bR], writes=[zt.R] if g == 0 else (), pwrites=() if g == 0 else [zt.R])
                        r0 = a0 + c * 128
                        zst = (lambda r0=r0, zt=zt: P.dma("sp", sc["Zs"][r0:r0 + 128, :], zt.t[:], reads=[zt.R], pwrites=[sR["Zs"]]))
                        if c >= 1:
                            zprev()
                        zprev = zst
                    zprev()
                    for cg in range(4):
                        w = ws.next(("in", l, 0, 8, C_V + cg * 512, 512))
                        for c in range(4):
                            bank, bR = take_ps()
                            P.mm([(lambda h, k=k, c=c, w=w: h.matmul(bank[:], lhsT=xt.t[:, k, c * 128:(c + 1) * 128],
                                                                     rhs=w.t[:, k, :], start=(k == 0), stop=(k == 7)))
                                  for k in range(8)], reads=[w.R, xt.R], writes=[bR])
                            en = evq()
                            P.op(en, copy_op(en, vtm.t[:, c, cg * 512:(cg + 1) * 512], bank[:]),
                                 reads=[bR], writes=[vtm.Rs[c]] if cg == 0 else (), pwrites=() if cg == 0 else [vtm.Rs[c]])
                    for c in range(4):
                        r0 = a0 + c * 128
                        km = ktm[c]
                        transposes(lambda blk, c=c: kTt.t[:, blk, c * 128:(c + 1) * 128], kTt.R, 8,
                                   lambda g, km=km: km.t[:, g * 512:(g + 1) * 512].rearrange("p (j c) -> p j c", j=4), km.R)
                        late.append(lambda r0=r0, km=km: P.dma("sp", sc["ktm"][r0:r0 + 128, :], km.t[:], reads=[km.R], pwrites=[sR["ktm"]]))
                        late.append(lambda r0=r0, c=c: P.dma("sp", sc["vtm"][r0:r0 + 128, :], vtm.t[:, c, :], reads=[vtm.Rs[c]],
                                                            pwrites=[sR["vtm"]]))
                    oprev = None
                    for c in range(4):
                        r0 = a0 + c * 128
                        km = ktm[c]
                        o = ot[c % 2]
                        st4 = (c % 2) * 4
                        def prep_a(c):
                            s4 = (c % 2) * 4
                            ret_prep(0, c, qTt, kTt, lambda h_, c=c: vtm.t[:, c, h_ * DV:(h_ + 1) * DV], vtm.Rs[c],
                                     vt[s4:s4 + 4], PT[s4:s4 + 4])
                        if c == 0:
                            prep_a(0)
                        ret_main(0, c, qTt, kTt, km.t, km.R, vt[st4:st4 + 4], PT[st4:st4 + 4], U, Sbf,
                                 lambda h_, o=o: o.t[:, h_ * DV:(h_ + 1) * DV], o.R, False,
                                 mid=(lambda c=c: prep_a(c + 1)) if c < 3 else None)
                        if oprev is not None:
                            oprev()
                        oprev = (lambda r0=r0, o=o: P.dma("sp", sc["of"][r0:r0 + 128, :], o.t[:], reads=[o.R], pwrites=[sR["of"]]))
                        if c == 1:
                            for fn in late:
                                fn()
                    oprev()
                P.barrier(label="A_" + nm)

        def phase_f1(nm, S):
            sc, sR = scr[nm], scrR[nm]
            n2c = S // 128
            nb = min(4, n2c)
            zview = sc["Zs"].rearrange("(a b) c -> a b c", b=n2c)
            with contextlib.ExitStack() as st:
                Zi = [T(nc, st, f"f1_z{i}", [128, nb, 2048], BF16) for i in range(2)]
                Mi = [T(nc, st, f"f1_m{i}", [128, nb, 3, 128], BF16) for i in range(2)]
                Yo = [T(nc, st, f"f1_y{i}", [128, nb, 2048], BF16, nres=nb) for i in range(2)]
                def loads(gi):
                    n20 = gi * nb
                    z, m = Zi[gi % 2], Mi[gi % 2]
                    P.dma("sp", z.t[:], zview[:, n20:n20 + nb, :], reads=[sR["Zs"]], writes=[z.R])
                    P.dma("sp", m.t[:], c_t1[S][n20:n20 + nb].rearrange("n p a b -> p n a b"), reads=[inR], writes=[m.R])

                loads(0)
                for gi, n20 in enumerate(range(0, n2c, nb)):
                    z, m, y = Zi[gi % 2], Mi[gi % 2], Yo[gi % 2]
                    if n20 + nb < n2c:
                        loads(gi + 1)
                    for j in range(nb):
                        for ri in range(2):
                            for cb in range(2):
                                bank, bR = take_ps()
                                zr = z.t[:, j, cb * 512:(cb + 1) * 512]
                                zi = z.t[:, j, 1024 + cb * 512:1024 + (cb + 1) * 512]
                                if ri == 0:
                                    ops = [(0, zr), (1, zi)]
                                else:
                                    ops = [(0, zi), (2, zr)]
                                P.mm([(lambda h, a=a, rhs=rhs, first=(ii == 0), bank=bank: h.matmul(
                                    bank[:], lhsT=m.t[:, j, a, :], rhs=rhs, start=first, stop=not first))
                                    for ii, (a, rhs) in enumerate(ops)], reads=[z.R, m.R], writes=[bR])
                                en = evq()
                                first = (ri == 0 and cb == 0)
                                P.op(en, copy_op(en, y.t[:, j, ri * 1024 + cb * 512:ri * 1024 + (cb + 1) * 512], bank[:]),
                                     reads=[bR], writes=[y.Rs[j]] if first else (), pwrites=() if first else [y.Rs[j]])
                    if gi > 0:
                        yprev()

                    def yst(y=y, n20=n20):
                        for ri in range(2):
                            P.dma("sp", sc["Ys"][:, ri, n20:n20 + nb, :], y.t[:, :, ri * 1024:(ri + 1) * 1024],
                                  reads=y.Rs, pwrites=[sR["Ys"]])
                    yprev = yst
                yprev()
                P.barrier(label="F1_" + nm)

        def phase_f3(nm, S):
            sc, sR = scr[nm], scrR[nm]
            n2c = S // 128
            K2 = 2 * n2c
            G = min(128, 512 // n2c)
            with contextlib.ExitStack() as st:
                Yi = [T(nc, st, f"f3_y{i}", [128, 128, 256], BF16) for i in range(2)]
                t3 = T(nc, st, "f3_t3", [128, n2c], BF16)
                fo = [T(nc, st, f"f3_o{i}", [128, S], BF16) for i in range(4)]
                P.dma("sp", t3.t[0:K2, :], c_t3[S][:, :], reads=[inR], writes=[t3.R])

                def loads(cp):
                    y = Yi[cp % 2]
                    for ri in range(2):
                        P.dma("sp", y.t[ri * n2c:(ri + 1) * n2c, :, :],
                              sc["Ys"][:, ri, :, cp * 256:(cp + 1) * 256].rearrange("k n c -> n k c"),
                              reads=[sR["Ys"]], writes=[y.R] if ri == 0 else (), pwrites=() if ri == 0 else [y.R])

                loads(0)
                fprev = None
                for cp in range(4):
                    y = Yi[cp % 2]
                    if cp + 1 < 4:
                        loads(cp + 1)
                    for j in range(2):
                        cc = cp * 2 + j
                        f = fo[cc % 4]
                        fv = f.t[:].rearrange("p (b a) -> p a b", a=128)
                        for g0 in range(0, 128, G):
                            bank, bR = take_ps()
                            P.mm([(lambda h, k1=k1, bank=bank: h.matmul(bank[:, (k1 - g0) * n2c:(k1 - g0 + 1) * n2c],
                                                                        lhsT=y.t[0:K2, k1, j * 128:(j + 1) * 128], rhs=t3.t[0:K2, :],
                                                                        start=True, stop=True)) for k1 in range(g0, g0 + G)],
                                 reads=[y.R, t3.R], writes=[bR])
                            en = evq()
                            P.op(en, copy_op(en, fv[:, g0:g0 + G, :], bank[:, 0:G * n2c].rearrange("p (a b) -> p a b", b=n2c)),
                                 reads=[bR], pwrites=[f.R])
                        if fprev is not None:
                            fprev()
                        fprev = (lambda cc=cc, f=f: P.dma("sp", sc["fT"][cc * 128:(cc + 1) * 128, :], f.t[:], reads=[f.R],
                                                          pwrites=[sR["fT"]]))
                fprev()
                P.barrier(label="F3_" + nm)

        def phase_b1(nm, S, l, par):
            sc, sR = scr[nm], scrR[nm]
            with contextlib.ExitStack() as st:
                xTt = [T(nc, st, f"b_xT{i}", [128, 8, TT], BF16) for i in range(2)]
                qTt = [T(nc, st, f"b_qT{i}", [128, 8, TT], BF16) for i in range(2)]
                kTt = [T(nc, st, f"b_kT{i}", [128, 8, TT], BF16) for i in range(2)]
                W = [T(nc, st, f"b_w{i}", [128, 8, 512], BF16) for i in range(2)]
                sg = T(nc, st, "b_sg", [128, 4, RV], BF16, nres=4)
                ktm = [T(nc, st, f"b_ktm{i}", [128, D], BF16) for i in range(3)]
                vtm = [T(nc, st, f"b_vtm{i}", [128, RV], BF16) for i in range(3)]
                ot = [T(nc, st, f"b_o{i}", [128, RV], F32) for i in range(3)]
                vt = [T(nc, st, f"b_vt{i}", [128, DV], BF16) for i in range(8)]
                PT = [T(nc, st, f"b_PT{i}", [128, 128], BF16) for i in range(8)]
                U = T(nc, st, "b_U", [128, 4, 2, DV], F32, nres=8)
                Sbf = T(nc, st, "b_Sbf", [128, 4, 2, DV], BF16, nres=8)
                rstages = [T(nc, st, f"b_rst{i}", [128, RV], BF16) for i in range(2)]
                rTts = [T(nc, st, f"b_rT{i}", [128, 16, TT], BF16) for i in range(2)]
                stt = T(nc, st, "b_stt", [128, 4, 6], F32)
                mv = T(nc, st, "b_mv", [128, 4, 2], F32)
                rs = T(nc, st, "b_rs", [128, 12], F32)
                init_state(U, Sbf)
                ntile = S // TT
                ws = WStream(W, [("in", l, 0, 8, C_G + cg * 512, 512) for cg in range(4)] * ntile)
                cidx = 0
                pending = [None]
                pending_prep = [None]

                def loads(it):
                    t = ntile - 1 - it
                    a0 = t * TT
                    for (tile_, key, resv) in ((xTt[it % 2], "xT", sR["xT"][par]), (qTt[it % 2], "qT", sR["qT"]),
                                               (kTt[it % 2], "kT", sR["kT"])):
                        src = sc["xT"][par] if key == "xT" else sc[key]
                        P.dma("sp", tile_.t[:], src[:, a0:a0 + TT].rearrange("(k p) t -> p k t", p=128),
                              reads=[resv], writes=[tile_.R])

                chunk_list = [(t_, c_) for t_ in reversed(range(ntile)) for c_ in reversed(range(4))]

                def chunk_loads(i):
                    t_, c_ = chunk_list[i]
                    r0_ = t_ * TT + c_ * 128
                    km_, vm_, o_ = ktm[i % 3], vtm[i % 3], ot[i % 3]
                    P.dma("sp", km_.t[:], sc["ktm"][r0_:r0_ + 128, :], reads=[sR["ktm"]], writes=[km_.R])
                    P.dma("sp", vm_.t[:], sc["vtm"][r0_:r0_ + 128, :], reads=[sR["vtm"]], writes=[vm_.R])
                    P.dma("sp", o_.t[:], sc["of"][r0_:r0_ + 128, :], reads=[sR["of"]], writes=[o_.R])

                loads(0)
                chunk_loads(0)
                for t in reversed(range(ntile)):
                    it = ntile - 1 - t
                    xt, qt, kt = xTt[it % 2], qTt[it % 2], kTt[it % 2]
                    a0 = t * TT
                    if it + 1 < ntile:
                        loads(it + 1)
                    for cg in range(4):
                        w = ws.next(("in", l, 0, 8, C_G + cg * 512, 512))
                        for c in range(4):
                            bank, bR = take_ps()
                            P.mm([(lambda h, k=k, c=c, w=w: h.matmul(bank[:], lhsT=xt.t[:, k, c * 128:(c + 1) * 128],
                                                                     rhs=w.t[:, k, :], start=(k == 0), stop=(k == 7)))
                                  for k in range(8)], reads=[w.R, xt.R], writes=[bR])
                            P.op("act", lambda h, c=c, cg=cg: h.activation(out=sg.t[:, c, cg * 512:(cg + 1) * 512], in_=bank[:],
                                                                           func=AF.Silu),
                                 reads=[bR], writes=[sg.Rs[c]] if cg == 0 else (), pwrites=() if cg == 0 else [sg.Rs[c]])
                    rTt = rTts[it % 2]
                    if pending_prep[0] is not None:
                        prep_b(pending_prep[0])
                        pending_prep[0] = None
                    for c in reversed(range(4)):
                        r0 = a0 + c * 128
                        km, vm, o = ktm[cidx % 3], vtm[cidx % 3], ot[cidx % 3]
                        rstage = rstages[cidx % 2]
                        st4 = (cidx % 2) * 4
                        cidx += 1
                        if cidx < len(chunk_list):
                            chunk_loads(cidx)
                        def prep_b(i):
                            t_, c_ = chunk_list[i]
                            it_ = ntile - 1 - t_
                            vm_ = vtm[i % 3]
                            s4 = (i % 2) * 4
                            ret_prep(1, c_, qTt[it_ % 2], kTt[it_ % 2], lambda h_, vm_=vm_: vm_.t[:, h_ * DV:(h_ + 1) * DV], vm_.R,
                                     vt[s4:s4 + 4], PT[s4:s4 + 4])
                        ci_ = cidx - 1
                        if ci_ == 0:
                            prep_b(0)
                        nxt_same_tile = (c > 0)
                        ret_main(1, c, qt, kt, km.t, km.R, vt[st4:st4 + 4], PT[st4:st4 + 4], U, Sbf,
                                 lambda h_, o=o: o.t[:, h_ * DV:(h_ + 1) * DV], o.R, True,
                                 mid=(lambda ci_=ci_: prep_b(ci_ + 1)) if nxt_same_tile else None)
                        if (not nxt_same_tile) and ci_ + 1 < len(chunk_list):
                            pending_prep[0] = ci_ + 1
                        if pending[0] is not None:
                            pending[0]()
                            pending[0] = None
                        for h_ in range(4):
                            P.op("dve", lambda h, h_=h_, o=o: h.bn_stats(out=stt.t[:, h_, :], in_=o.t[:, h_ * DV:(h_ + 1) * DV]),
                                 reads=[o.R], pwrites=[stt.R])
                        for h_ in range(4):
                            P.op("dve", lambda h, h_=h_: h.bn_aggr(out=mv.t[:, h_, :], in_=stt.t[:, h_, :]),
                                 reads=[stt.R], pwrites=[mv.R])
                        P.op("act", lambda h: h.activation(out=rs.t[:, 0:4], in_=mv.t[:, :, 1], func=AF.Sqrt, bias=epsc.t[:, 1:2]),
                             reads=[mv.R, epsc.R], writes=[rs.R])
                        P.op("dve", lambda h: h.reciprocal(out=rs.t[:, 4:8], in_=rs.t[:, 0:4]), reads=[rs.R], writes=[rs.R])
                        P.op("dve", lambda h: h.scalar_tensor_tensor(out=rs.t[:, 8:12], in0=mv.t[:, :, 0], scalar=-1.0, in1=rs.t[:, 4:8],
                                                                     op0=ALU.mult, op1=ALU.mult), reads=[mv.R, rs.R], writes=[rs.R])
                        for h_ in range(4):
                            osl = o.t[:, h_ * DV:(h_ + 1) * DV]
                            P.op("act", lambda h, h_=h_, osl=osl: h.activation(out=osl, in_=osl, func=AF.Identity,
                                                                               scale=rs.t[:, 4 + h_:5 + h_], bias=rs.t[:, 8 + h_:9 + h_]),
                                 reads=[o.R, rs.R], writes=[o.R])
                        P.op("pool", lambda h, o=o, c=c, rstage=rstage: h.tensor_tensor(out=rstage.t[:, 0:1024], in0=o.t[:, 0:1024],
                                                                                        in1=sg.t[:, c, 0:1024], op=ALU.mult),
                             reads=[o.R, sg.Rs[c]], writes=[rstage.R])
                        P.op("pool", lambda h, o=o, c=c, rstage=rstage: h.tensor_tensor(out=rstage.t[:, 1024:2048], in0=o.t[:, 1024:2048],
                                                                                        in1=sg.t[:, c, 1024:2048], op=ALU.mult),
                             reads=[o.R, sg.Rs[c]], pwrites=[rstage.R])

                        def post_b(c=c, rstage=rstage, rTt=rTt, a0=a0):
                            transposes(lambda blk: rstage.t[:, blk * 128:(blk + 1) * 128], rstage.R, 16,
                                       lambda g: rTt.t[:, g * 4:(g + 1) * 4, c * 128:(c + 1) * 128], rTt.R)
                            if c == 0:
                                P.dma("sp", sc["rT"][:, a0:a0 + TT].rearrange("(k p) t -> p k t", p=128), rTt.t[:],
                                      reads=[rTt.R], pwrites=[sR["rT"]])
                        pending[0] = post_b
                if pending[0] is not None:
                    pending[0]()
                P.barrier(label="B1_" + nm)

        def phase_b2(nm, S, l, par):
            sc, sR = scr[nm], scrR[nm]
            xres_src = xin[nm] if l == 0 else sc["xres"][par]
            xres_R = inR if l == 0 else sR["xres"][par]
            with contextlib.ExitStack() as st:
                xTh = [T(nc, st, f"c_xT{i}", [128, 8, TT + 1], BF16) for i in range(2)]
                fTt = T(nc, st, "c_fT", [128, 8, TT], BF16)
                rTt = T(nc, st, "c_rT", [128, 16, TT], BF16)
                cTt = T(nc, st, "c_cT", [128, 8, TT], BF16)
                W = [T(nc, st, f"c_w{i}", [128, 8, 512], BF16) for i in range(7)]
                btmp = [T(nc, st, f"c_bt{i}", [128, TT], F32) for i in range(2)]
                lnp = load_lnp(st, l, 0)
                ctmp = [T(nc, st, f"c_ct{i}", [128, TT], F32) for i in range(2)]
                ub = [T(nc, st, f"c_ub{i}", [128, TT + 2], F32) for i in range(2)]
                ytmp = [T(nc, st, f"c_yt{i}", [128, TT], F32) for i in range(2)]
                saved = T(nc, st, "c_sv", [128, 8, 2], F32, nres=8)
                u0 = T(nc, st, "c_u0", [128, 8, 2], F32)
                m = T(nc, st, "c_m", [128, 8, TT], F32, nres=8)
                mT = T(nc, st, "c_mT", [128, 8, TT], BF16)
                sgt = [T(nc, st, f"c_sg{i}", [128, TT], F32) for i in range(2)]
                tmp2 = [T(nc, st, f"c_t2{i}", [128, TT], F32) for i in range(2)]
                pres = [T(nc, st, f"c_pre{i}", [128, 4, D], F32, nres=4) for i in range(1)]
                x1T = T(nc, st, "c_x1T", [128, 8, TT], BF16)
                stt = T(nc, st, "c_stt", [128, 2, 6], F32)
                mv = T(nc, st, "c_mv", [128, 2], F32)
                rs = T(nc, st, "c_rs", [128, 2], F32)
                xb4 = T(nc, st, "c_xb4", [128, 4, D], BF16, nres=4)
                ntile = S // TT
                per_tile = []
                for cg in range(2):
                    per_tile += [("in", l, 0, 8, cc_ + cg * 512, 512) for cc_ in (C_CC, C_CX, C_CB)]
                for nk_, wkey_, gcol_ in ((8, "fo", C_GF), (16, "ro", C_GR), (8, "co", C_GC)):
                    for cg in range(2):
                        per_tile.append(("in", l, 0, 8, gcol_ + cg * 512, 512))
                        per_tile += [(wkey_, l, kb * 1024, 8, cg * 512, 512) for kb in range(nk_ // 8)]
                per_tile += [("o", l, 0, 8, cg * 512, 512) for cg in range(2)]
                ws = WStream(W, per_tile * ntile, hold=3)

                def next_w(*spec):
                    return ws.next(spec)

                def loads(t):
                    a0 = t * TT
                    xt = xTh[t % 2]
                    last = (t == ntile - 1)
                    ncol = TT if last else TT + 1
                    if last:
                        P.op("pool", lambda h: h.memset(xt.t[:, :, TT:TT + 1], 0.0), writes=[xt.R])
                    P.dma("sp", xt.t[:, :, 0:ncol], sc["xT"][par][:, a0:a0 + ncol].rearrange("(k p) t -> p k t", p=128),
                          reads=[sR["xT"][par]], writes=() if last else [xt.R], pwrites=[xt.R] if last else ())

                def load_pre(t):
                    a0 = t * TT
                    P.dma("sp", pres[0].t[:], xres_src[a0:a0 + TT, :].rearrange("(c p) d -> p c d", p=128),
                          reads=[xres_R], writes=pres[0].Rs)

                pending = [None]
                for i in range(8):
                    P.op("pool", lambda h, i=i: h.memset(saved.t[:, i, :], 0.0), writes=[saved.Rs[i]])
                loads(0)
                for t in range(ntile):
                    a0 = t * TT
                    xt = xTh[t % 2]
                    if t + 1 < ntile:
                        loads(t + 1)
                    pre = pres[0]
                    for cg in range(2):
                        wc = next_w("in", l, 0, 8, C_CC + cg * 512, 512)
                        wx = next_w("in", l, 0, 8, C_CX + cg * 512, 512)
                        wb_ = next_w("in", l, 0, 8, C_CB + cg * 512, 512)
                        for rb in range(4):
                            blk = cg * 4 + rb
                            ct, u, yt = ctmp[blk % 2], ub[blk % 2], ytmp[blk % 2]
                            wsl = slice(rb * 128, (rb + 1) * 128)
                            if t == 0:
                                bk0, bk0R = take_ps()
                                P.mm([(lambda h, k=k, w_=w_, col=col, bk0=bk0: h.matmul(
                                    bk0[:, col:col + 1], lhsT=w_.t[:, k, wsl], rhs=xt.t[:, k, 0:1],
                                    start=(k == 0), stop=(k == 7))) for col, w_ in ((0, wc), (1, wx)) for k in range(8)],
                                    reads=[wc.R, wx.R, xt.R], writes=[bk0R])
                                P.op("act", lambda h, bk0=bk0: h.activation(out=u0.t[:, blk, 0:1], in_=bk0[:, 0:1], func=AF.Copy),
                                     reads=[bk0R], writes=[u0.R])
                                P.op("dve", lambda h, bk0=bk0, blk=blk: h.tensor_tensor(
                                    out=saved.t[:, blk, 1:2], in0=u0.t[:, blk, 0:1], in1=bk0[:, 1:2], op=ALU.mult),
                                    reads=[bk0R, u0.R], writes=[saved.Rs[blk]])
                            bC, bCR = take_ps()
                            P.mm([(lambda h, k=k, bC=bC: h.matmul(bC[:], lhsT=wc.t[:, k, wsl], rhs=xt.t[:, k, 1:TT + 1],
                                                                  start=(k == 0), stop=(k == 7))) for k in range(8)],
                                 reads=[wc.R, xt.R], writes=[bCR])
                            bX, bXR = take_ps()
                            P.mm([(lambda h, k=k, bX=bX: h.matmul(bX[:], lhsT=wx.t[:, k, wsl], rhs=xt.t[:, k, 1:TT + 1],
                                                                  start=(k == 0), stop=(k == 7))) for k in range(8)],
                                 reads=[wx.R, xt.R], writes=[bXR])
                            bB, bBR = take_ps()
                            P.mm([(lambda h, k=k, bB=bB: h.matmul(bB[:], lhsT=wb_.t[:, k, wsl], rhs=xt.t[:, k, 0:TT],
                                                                  start=(k == 0), stop=(k == 7))) for k in range(8)],
                                 reads=[wb_.R, xt.R], writes=[bBR])
                            P.op("act", lambda h, ct=ct, bC=bC: h.activation(out=ct.t[:], in_=bC[:], func=AF.Copy),
                                 reads=[bCR], writes=[ct.R])
                            bt = btmp[blk % 2]
                            P.op("act", lambda h, bt=bt, bB=bB: h.activation(out=bt.t[:], in_=bB[:], func=AF.Copy),
                                 reads=[bBR], writes=[bt.R])
                            P.op("act", lambda h, u=u, blk=blk: h.activation(out=u.t[:, 0:2], in_=saved.t[:, blk, :], func=AF.Copy),
                                 reads=[saved.Rs[blk]], writes=[u.R])
                            P.op("dve", lambda h, u=u, ct=ct, bX=bX: h.tensor_tensor(out=u.t[:, 2:TT + 2], in0=ct.t[:], in1=bX[:],
                                                                                      op=ALU.mult),
                                 reads=[ct.R, bXR], pwrites=[u.R])
                            P.op("act", lambda h, u=u, blk=blk: h.activation(out=saved.t[:, blk, :], in_=u.t[:, TT:TT + 2], func=AF.Copy),
                                 reads=[u.R], writes=[saved.Rs[blk]])
                            P.op("act", lambda h, u=u, yt=yt, blk=blk: h.activation(
                                out=yt.t[:], in_=u.t[:, 0:TT], func=AF.Copy, scale=cw.t[:, blk, 0:1]),
                                reads=[u.R, cw.R], writes=[yt.R])
                            for j in (1, 2):
                                P.op("dve", lambda h, u=u, yt=yt, blk=blk, j=j: h.scalar_tensor_tensor(
                                    out=yt.t[:], in0=u.t[:, j:TT + j], scalar=cw.t[:, blk, j:j + 1], in1=yt.t[:],
                                    op0=ALU.mult, op1=ALU.add), reads=[u.R, cw.R, yt.R], writes=[yt.R])
                            P.op("dve", lambda h, yt=yt, bt=bt, blk=blk: h.tensor_tensor(out=cTt.t[:, blk, :], in0=yt.t[:], in1=bt.t[:],
                                                                                         op=ALU.mult),
                                 reads=[yt.R, bt.R], pwrites=[cTt.R])
                            if blk == 6:
                                if pending[0] is not None:
                                    pending[0]()
                                    pending[0] = None
                                load_pre(t)
                            if blk == 2:
                                P.dma("sp", fTt.t[:], sc["fT"][:, a0:a0 + TT].rearrange("(k p) t -> p k t", p=128),
                                      reads=[sR["fT"]], writes=[fTt.R])
                                P.dma("sp", rTt.t[:], sc["rT"][:, a0:a0 + TT].rearrange("(k p) t -> p k t", p=128),
                                      reads=[sR["rT"]], writes=[rTt.R])
                    for bi, (src_t, nk, wkey, gcol) in enumerate(((fTt, 8, "fo", C_GF), (rTt, 16, "ro", C_GR), (cTt, 8, "co", C_GC))):
                        for cg in range(2):
                            wg = next_w("in", l, 0, 8, gcol + cg * 512, 512)
                            wos = []
                            for kb in range(nk // 8):
                                wos.append(next_w(wkey, l, kb * 1024, 8, cg * 512, 512))
                            for rb in range(4):
                                blk = cg * 4 + rb
                                wsl = slice(rb * 128, (rb + 1) * 128)
                                bG, bGR = take_ps()
                                P.mm([(lambda h, k=k, bG=bG: h.matmul(bG[:], lhsT=wg.t[:, k, wsl], rhs=xt.t[:, k, 0:TT],
                                                                      start=(k == 0), stop=(k == 7))) for k in range(8)],
                                     reads=[wg.R, xt.R], writes=[bGR])
                                bO, bOR = take_ps()
                                P.mm([(lambda h, kk=kk, bO=bO: h.matmul(bO[:], lhsT=wos[kk // 8].t[:, kk % 8, wsl],
                                                                        rhs=src_t.t[:, kk, :], start=(kk == 0), stop=(kk == nk - 1)))
                                      for kk in range(nk)], reads=[w_.R for w_ in wos] + [src_t.R], writes=[bOR])
                                sg_ = sgt[blk % 2]
                                P.op("act", lambda h, sg_=sg_, bG=bG: h.activation(out=sg_.t[:], in_=bG[:], func=AF.Sigmoid),
                                     reads=[bGR], writes=[sg_.R])
                                if bi == 0:
                                    P.op("dve", lambda h, sg_=sg_, bO=bO, blk=blk: h.tensor_tensor(out=m.t[:, blk, :], in0=sg_.t[:],
                                                                                                   in1=bO[:], op=ALU.mult),
                                         reads=[sg_.R, bOR], writes=[m.Rs[blk]])
                                else:
                                    t2 = tmp2[blk % 2]
                                    P.op("dve", lambda h, sg_=sg_, bO=bO, t2=t2: h.tensor_tensor(out=t2.t[:], in0=sg_.t[:], in1=bO[:],
                                                                                                 op=ALU.mult),
                                         reads=[sg_.R, bOR], writes=[t2.R])
                                    if bi == 1:
                                        P.op("dve", lambda h, t2=t2, blk=blk: h.tensor_tensor(out=m.t[:, blk, :], in0=m.t[:, blk, :],
                                                                                               in1=t2.t[:], op=ALU.add),
                                             reads=[t2.R, m.Rs[blk]], writes=[m.Rs[blk]])
                                    else:
                                        P.op("dve", lambda h, t2=t2, blk=blk: h.tensor_tensor(out=mT.t[:, blk, :], in0=m.t[:, blk, :],
                                                                                               in1=t2.t[:], op=ALU.add),
                                             reads=[t2.R, m.Rs[blk]], writes=[mT.R] if blk == 0 else (),
                                             pwrites=() if blk == 0 else [mT.R])
                    for cg in range(2):
                        w = next_w("o", l, 0, 8, cg * 512, 512)
                        for c in range(4):
                            bank, bR = take_ps()
                            P.mm([(lambda h, k=k, c=c, bank=bank: h.matmul(bank[:], lhsT=mT.t[:, k, c * 128:(c + 1) * 128],
                                                                           rhs=w.t[:, k, :], start=(k == 0), stop=(k == 7)))
                                  for k in range(8)], reads=[w.R, mT.R], writes=[bR])
                            psl = pre.t[:, c, cg * 512:(cg + 1) * 512]
                            P.op("dve", lambda h, psl=psl, bank=bank: h.scalar_tensor_tensor(out=psl, in0=psl, scalar=ALPHA, in1=bank[:],
                                                                                            op0=ALU.mult, op1=ALU.add),
                                 reads=[bR, pre.Rs[c]], writes=[pre.Rs[c]])
                    if debug:
                        P.dma("sp", sc["dmT"][:, a0:a0 + TT].rearrange("(k p) t -> p k t", p=128), mT.t[:], reads=[mT.R], pwrites=[sR["dmT"]])
                        P.dma("sp", sc["dcT"][:, a0:a0 + TT].rearrange("(k p) t -> p k t", p=128), cTt.t[:], reads=[cTt.R], pwrites=[sR["dcT"]])
                        P.dma("sp", sc["dm"][:, a0:a0 + TT].rearrange("(k p) t -> p k t", p=128), m.t[:], reads=m.Rs, pwrites=[sR["dm"]])
                        P.dma("sp", sc["dpre"][a0:a0 + TT, :].rearrange("(c p) d -> p c d", p=128), pre.t[:], reads=pre.Rs, pwrites=[sR["dpre"]])

                    stores_fn = ln_math(lnp, pre, sc["x1"][a0:a0 + TT, :], sR["x1"], xb4, stt, mv, rs)

                    def post(a0=a0, stores_fn=stores_fn):
                        stores_fn()
                        ln_T(xb4, x1T)
                        P.dma("sp", sc["x1T"][:, a0:a0 + TT].rearrange("(k p) t -> p k t", p=128), x1T.t[:],
                              reads=[x1T.R], pwrites=[sR["x1T"]])
                    pending[0] = post
                if pending[0] is not None:
                    pending[0]()
                P.barrier(label="B2_" + nm)

        def phase_c(nm, S, l, par, is_last):
            sc, sR = scr[nm], scrR[nm]
            dst = yout[nm] if is_last else sc["xres"][1 - par]
            dstR = yR[nm] if is_last else sR["xres"][1 - par]
            with contextlib.ExitStack() as st:
                x1T = [T(nc, st, f"d_x1T{i}", [128, 8, TT], BF16) for i in range(2)]
                pre = [T(nc, st, f"d_pre{i}", [128, 4, D], F32, nres=4) for i in range(2)]
                W = [T(nc, st, f"d_w{i}", [128, 8, 512], BF16) for i in range(8)]
                lnp = load_lnp(st, l, 1)
                hT = T(nc, st, "d_hT", [128, 22, TT], BF16)
                stmp = [T(nc, st, f"d_st{i}", [128, TT], F32) for i in range(2)]
                xTn = T(nc, st, "d_xTn", [128, 8, TT], BF16)
                stt = T(nc, st, "d_stt", [128, 2, 6], F32)
                mv = T(nc, st, "d_mv", [128, 2], F32)
                rs = T(nc, st, "d_rs", [128, 2], F32)
                xb4 = T(nc, st, "d_xb4", [128, 4, D], BF16, nres=4)
                ntile = S // TT
                per_tile = []
                for cg in range(6):
                    wd_ = 512 if cg < 5 else 256
                    per_tile += [("fi", l, 0, 8, cg * 512, wd_), ("fi", l, 0, 8, FF + cg * 512, wd_)]
                for cg in range(2):
                    per_tile += [("fout", l, kb * 1024, nkb, cg * 512, 512) for kb, nkb in enumerate((8, 8, 6))]
                ws = WStream(W, per_tile * ntile, hold=2)

                def next_w(*spec):
                    return ws.next(spec)

                def loads(t):
                    a0 = t * TT
                    xt, pr = x1T[t % 2], pre[t % 2]
                    P.dma("sp", xt.t[:], sc["x1T"][:, a0:a0 + TT].rearrange("(k p) t -> p k t", p=128),
                          reads=[sR["x1T"]], writes=[xt.R])

                def load_pre(t):
                    a0 = t * TT
                    pr = pre[t % 2]
                    P.dma("sp", pr.t[:], sc["x1"][a0:a0 + TT, :].rearrange("(c p) d -> p c d", p=128),
                          reads=[sR["x1"]], writes=pr.Rs)

                loads(0)
                load_pre(0)
                pending = [None]
                for t in range(ntile):
                    a0 = t * TT
                    xt, pr = x1T[t % 2], pre[t % 2]
                    if t + 1 < ntile:
                        loads(t + 1)
                    for cg in range(6):
                        wd = 512 if cg < 5 else 256
                        wgt = next_w("fi", l, 0, 8, cg * 512, wd)
                        wup = next_w("fi", l, 0, 8, FF + cg * 512, wd)
                        for rb in range(wd // 128):
                            j = cg * 4 + rb
                            wsl = slice(rb * 128, (rb + 1) * 128)
                            bG, bGR = take_ps()
                            P.mm([(lambda h, k=k, bG=bG: h.matmul(bG[:], lhsT=wgt.t[:, k, wsl], rhs=xt.t[:, k, :],
                                                                  start=(k == 0), stop=(k == 7))) for k in range(8)],
                                 reads=[wgt.R, xt.R], writes=[bGR])
                            bU, bUR = take_ps()
                            P.mm([(lambda h, k=k, bU=bU: h.matmul(bU[:], lhsT=wup.t[:, k, wsl], rhs=xt.t[:, k, :],
                                                                  start=(k == 0), stop=(k == 7))) for k in range(8)],
                                 reads=[wup.R, xt.R], writes=[bUR])
                            s_ = stmp[j % 2]
                            P.op("act", lambda h, s_=s_, bG=bG: h.activation(out=s_.t[:], in_=bG[:], func=AF.Silu),
                                 reads=[bGR], writes=[s_.R])
                            P.op("dve", lambda h, s_=s_, bU=bU, j=j: h.tensor_tensor(out=hT.t[:, j, :], in0=s_.t[:], in1=bU[:],
                                                                                     op=ALU.mult),
                                 reads=[s_.R, bUR], pwrites=[hT.R])
                            if j == 10:
                                if pending[0] is not None:
                                    pending[0]()
                                    pending[0] = None
                                if t + 1 < ntile:
                                    load_pre(t + 1)
                    for cg in range(2):
                        banks = [take_ps() for _ in range(4)]
                        for kb, nkb in enumerate((8, 8, 6)):
                            w = next_w("fout", l, kb * 1024, nkb, cg * 512, 512)
                            for c in range(4):
                                bank, bR = banks[c]
                                fns = [(lambda h, k=k, c=c, bank=bank, kb=kb, nkb=nkb: h.matmul(
                                    bank[:], lhsT=hT.t[:, kb * 8 + k, c * 128:(c + 1) * 128], rhs=w.t[:, k, :],
                                    start=(kb == 0 and k == 0), stop=(kb == 2 and k == nkb - 1))) for k in range(nkb)]
                                if kb == 0:
                                    P.mm(fns, reads=[w.R, hT.R], writes=[bR])
                                else:
                                    P.mm(fns, reads=[w.R, hT.R], pwrites=[bR])
                        for c in range(4):
                            bank, bR = banks[c]
                            psl = pr.t[:, c, cg * 512:(cg + 1) * 512]
                            P.op("dve", lambda h, psl=psl, bank=bank: h.scalar_tensor_tensor(out=psl, in0=psl, scalar=ALPHA, in1=bank[:],
                                                                                            op0=ALU.mult, op1=ALU.add),
                                 reads=[bR, pr.Rs[c]], writes=[pr.Rs[c]])

                    stores_fn = ln_math(lnp, pr, dst[a0:a0 + TT, :], dstR, xb4, stt, mv, rs)

                    def post(a0=a0, stores_fn=stores_fn):
                        stores_fn()
                        if is_last:
                            return
                        ln_T(xb4, xTn)
                        if not is_last:
                            P.dma("sp", sc["xT"][1 - par][:, a0:a0 + TT].rearrange("(k p) t -> p k t", p=128), xTn.t[:],
                                  reads=[xTn.R], pwrites=[sR["xT"][1 - par]])
                    pending[0] = post
                if pending[0] is not None:
                    pending[0]()
                P.barrier(label="C_" + nm)

        for nm, S in seqs:
            phase_x0(nm, S)
        for l in range(L):
            layer_consts(l)
            par = l % 2
            for nm, S in seqs:
                phase_a(nm, S, l, par)
                phase_f1(nm, S)
                phase_f3(nm, S)
                phase_b1(nm, S, l, par)
                phase_b2(nm, S, l, par)
                phase_c(nm, S, l, par, l == L - 1)
        P.barrier(include_bg=True)
        build.last_ninst = P.ninst
        build.plog = P.plog
    return nc


def run(inputs, seq_map, L, n_cores, debug=False, trace=False):
    seqs = [(nm, a.shape[0]) for nm, a in seq_map[0].items()]
    nc = build(seqs, L, debug=debug)
    smax = max(S for _, S in seqs)
    consts = make_consts([S for _, S in seqs], smax)
    shared = {
        "w_in": inputs["w_in"], "w_fourier_out": inputs["w_fourier_out"], "w_ret_out": inputs["w_ret_out"],
        "w_conv_out": inputs["w_conv_out"], "w_o": inputs["w_o"], "w_ffn_in": inputs["w_ffn_in"],
        "w_ffn_out": inputs["w_ffn_out"], "ret_decay_logit": inputs["ret_decay_logit"].reshape(L, 8),
        "conv_w": inputs["conv_w"], "ln_gain": inputs["ln_gain"], "ln_bias": inputs["ln_bias"],
    }
    shared = {k: np.ascontiguousarray(np.asarray(v, dtype=np.float32)) for k, v in shared.items()}
    shared.update(consts)
    in_maps = []
    for c in range(n_cores):
        m = dict(shared)
        for nm, a in seq_map[c].items():
            m[f"x_{nm}"] = np.ascontiguousarray(np.asarray(a, dtype=np.float32))
        in_maps.append(m)
    res = run_bass_kernel_spmd(nc, in_maps, core_ids=list(range(n_cores)), **({"trace": True} if trace else {}))
    return res


def kernel(x_prompt, x_sample, w_in, ret_decay_logit, conv_w, w_fourier_out, w_ret_out,
           w_conv_out, w_o, ln_gain, ln_bias, w_ffn_in, w_ffn_out):
    inputs = dict(w_in=w_in, ret_decay_logit=ret_decay_logit, conv_w=conv_w, w_fourier_out=w_fourier_out,
                  w_ret_out=w_ret_out, w_conv_out=w_conv_out, w_o=w_o, ln_gain=ln_gain, ln_bias=ln_bias,
                  w_ffn_in=w_ffn_in, w_ffn_out=w_ffn_out)
    x_prompt = np.asarray(x_prompt)
    x_sample = np.asarray(x_sample)
    n = 8
    nb_p = x_prompt.shape[0]
    seq_map = [{"s": x_sample[c], "p": x_prompt[c % nb_p]} for c in range(n)]
    res = run(inputs, seq_map, int(np.asarray(w_in).shape[0]), n)
    y_sample = np.stack([res.results[c]["y_s"] for c in range(n)], axis=0).astype(np.float32)
    y_prompt = np.stack([res.results[c]["y_p"] for c in range(nb_p)], axis=0).astype(np.float32)
    return (y_prompt, y_sample)
```
